# Optimizing a Trainium2 kernel written in Bass

```python
import math
import jax
import jax.numpy as jnp
from jax import lax
import numpy as np

D_MODEL = 1024
BATCH = 8
SEQ = 4096
DEPTH = 4

D_A = D_MODEL // 2
H_A = 4
DH_A = D_A // H_A
CHUNK_A = 64
D_B = D_MODEL // 2
HY_BANDS = 16
HY_EMB = 2 * HY_BANDS + 1
HY_HID = 64
HY_TARGET = 1e-2
HY_FAST_PCT = 0.3
HY_SLOW_PCT = 1.5
D_C = D_MODEL
HEAD_C = 64
H_C = D_C // HEAD_C
LORA_W = 64
LORA_A = 64
LNX_EPS = 64e-5
SHORT_CONV = 3
NORM_EPS = 1e-6
HEAD_NORM_EPS = 1e-5
N_SHIFT_C = 3 * D_C + 2 * LORA_W + 2 * LORA_A
N_IN = 2 * D_A + 4 * D_B + N_SHIFT_C + D_C + 3 * D_MODEL

kernel_name = 'hybrid_mlstm_hyena_rwkv7_encoder'


def rmsnorm(x, g):
    xf = x.astype(jnp.float32)
    y = xf * lax.rsqrt(jnp.mean(xf * xf, axis=-1, keepdims=True) + NORM_EPS)
    return (y * g.astype(jnp.float32)).astype(x.dtype)


def head_norm(h, eps):
    h = h.astype(jnp.float32)
    mu = jnp.mean(h, axis=-1, keepdims=True)
    var = jnp.mean(jnp.square(h - mu), axis=-1, keepdims=True)
    return (h - mu) * lax.rsqrt(var + eps)


def split_cols(t, sizes):
    offs = np.cumsum(sizes)[:-1].tolist()
    return jnp.split(t, offs, axis=-1)


def short_conv(u, w, b):
    ch = u.shape[-1]
    y = lax.conv_general_dilated(u, w[:, None, :].astype(u.dtype), window_strides=(1,),
                                 padding=[(SHORT_CONV // 2, SHORT_CONV // 2)],
                                 dimension_numbers=('NWC', 'WIO', 'NWC'),
                                 feature_group_count=ch)
    return y + b


def neighbour_shift(u, mu):
    up = jnp.pad(u, ((0, 0), (1, 1), (0, 0)))
    nb = 0.5 * (up[:, :-2] + up[:, 2:])
    return u + mu * (nb - u)


def mlstm_chunkwise(q, k, v, log_i, log_f):
    B, S, H, Dh = q.shape
    L = CHUNK_A
    NC = S // L
    f32 = jnp.float32
    k = k * (Dh ** -0.5)

    def blocks(t):
        return t.reshape(B, NC, L, H, -1).transpose(1, 0, 3, 2, 4)

    qc, kc, vc = blocks(q), blocks(k), blocks(v)
    li = blocks(log_i[..., None])[..., 0]
    bc = jnp.cumsum(blocks(log_f[..., None])[..., 0], axis=-1)
    lower = jnp.tril(jnp.ones((L, L), dtype=bool))

    def step(carry, inp):
        C, n, m = carry
        qj, kj, vj, lij, bj = inp
        qf, kf, vf = qj.astype(f32), kj.astype(f32), vj.astype(f32)
        d = jnp.where(lower, bj[..., :, None] - bj[..., None, :] + lij[..., None, :], -jnp.inf)
        inter = bj + m[..., None]
        m_t = jnp.maximum(inter, jnp.max(d, axis=-1))
        w_intra = jnp.exp(d - m_t[..., None])
        w_inter = jnp.exp(inter - m_t)
        s = jnp.einsum('bhtd,bhsd->bhts', qf, kf) * w_intra
        num = (jnp.einsum('bhts,bhsd->bhtd', s, vf)
               + w_inter[..., None] * jnp.einsum('bhde,bhte->bhtd', C, qf))
        den = jnp.sum(s, axis=-1) + w_inter * jnp.einsum('bhe,bhte->bht', n, qf)
        h = num / jnp.maximum(jnp.abs(den), jnp.exp(-m_t))[..., None]
        g = bj[..., -1]
        a = g[..., None] - bj + lij
        m_new = jnp.maximum(g + m, jnp.max(a, axis=-1))
        decay = jnp.exp(g + m - m_new)
        wk = jnp.exp(a - m_new[..., None])
        C = decay[..., None, None] * C + jnp.einsum('bhsd,bhse->bhde', vf * wk[..., None], kf)
        n = decay[..., None] * n + jnp.einsum('bhs,bhse->bhe', wk, kf)
        return (C, n, m_new), h

    init = (jnp.zeros((B, H, Dh, Dh), f32), jnp.zeros((B, H, Dh), f32),
            jnp.full((B, H), -jnp.inf, f32))
    _, hs = lax.scan(step, init, (qc, kc, vc, li, bc))
    return hs.transpose(1, 0, 3, 2, 4).reshape(B, S, H, Dh)


def mlstm_branch(xa, za, conv_w, conv_b, wq, wk, wv, gate_w, gate_b, norm_g, skip):
    B, S, _ = xa.shape
    xc = jax.nn.silu(short_conv(xa, conv_w, conv_b))
    q = jnp.einsum('bshd,hde->bshe', xc.reshape(B, S, H_A, DH_A), wq)
    k = jnp.einsum('bshd,hde->bshe', xc.reshape(B, S, H_A, DH_A), wk)
    v = jnp.einsum('bshd,hde->bshe', xa.reshape(B, S, H_A, DH_A), wv)
    gin = jnp.concatenate([q.reshape(B, S, D_A), k.reshape(B, S, D_A), v.reshape(B, S, D_A)], axis=-1)
    gates = (gin @ gate_w + gate_b).astype(jnp.float32).reshape(B, S, 2, 2, H_A)
    log_i = gates[:, :, :, 0]
    log_f = jax.nn.log_sigmoid(gates[:, :, :, 1])
    h_fwd = mlstm_chunkwise(q, k, v, log_i[:, :, 0], log_f[:, :, 0])
    rev = lambda t: jnp.flip(t, axis=1)
    h_bwd = rev(mlstm_chunkwise(rev(q), rev(k), rev(v), rev(log_i[:, :, 1]), rev(log_f[:, :, 1])))
    h = head_norm(h_fwd + h_bwd, HEAD_NORM_EPS).reshape(B, S, D_A) * norm_g
    return ((h + skip * xc) * jax.nn.silu(za)).astype(xa.dtype)


def hyena_positional_features(L):
    t = jnp.linspace(0.0, 1.0, L, dtype=jnp.float32)
    w = 2.0 * math.pi * jnp.arange(L, dtype=jnp.float32) / L
    f = jnp.linspace(1e-4, HY_BANDS - 1, HY_BANDS, dtype=jnp.float32)
    z = w[:, None] * f[None, :]
    feats = jnp.concatenate([t[:, None], jnp.cos(z), -jnp.sin(z)], axis=-1)
    return feats, t


def hyena_filter(feats, t, w1, b1, w2, b2, w3, b3, freq, w_out, decay):
    L = feats.shape[0]
    act = lambda z: jnp.sin(freq * z)
    hid = act(feats @ w1 + b1)
    hid = act(hid @ w2 + b2)
    hid = act(hid @ w3 + b3)
    filt = (hid @ w_out).reshape(L, 2, D_B)
    return filt * jnp.exp(-t[:, None, None] * jnp.abs(decay))


def bidirectional_fftconv(z, h_fwd, h_bwd):
    L = z.shape[1]
    n = 2 * L
    k_full = jnp.concatenate([h_fwd, jnp.zeros_like(h_fwd[:1]), h_bwd[:0:-1]], axis=0)
    k_f = jnp.fft.rfft(k_full.astype(jnp.float32), n=n, axis=0)
    z_f = jnp.fft.rfft(z.astype(jnp.float32), n=n, axis=1)
    return jnp.fft.irfft(z_f * k_f[None], n=n, axis=1)[:, :L]


def hyena_branch(u, zb, conv_w, conv_b, filt, bias):
    uc = short_conv(u, conv_w, conv_b)
    v, x0, x1 = jnp.split(uc, 3, axis=-1)
    z = (v * x1).astype(jnp.float32)
    z = bidirectional_fftconv(z, filt[:, 0], filt[:, 1]) + bias * z
    return (x0 * z * jax.nn.silu(zb)).astype(u.dtype)


def rwkv7_scan(r, w, k, v, kk, a, reverse):
    B, S, H, N = r.shape

    def step(state, inp):
        r_t, w_t, k_t, v_t, kk_t, a_t = inp
        sa = jnp.einsum('bhij,bhj->bhi', state, -kk_t)
        state = (state * w_t[:, :, None, :] + sa[..., None] * (kk_t * a_t)[:, :, None, :]
                 + v_t[..., None] * k_t[:, :, None, :])
        return state, jnp.einsum('bhij,bhj->bhi', state, r_t)

    xs = tuple(jnp.moveaxis(t.astype(jnp.float32), 1, 0) for t in (r, w, k, v, kk, a))
    _, ys = lax.scan(step, jnp.zeros((B, H, N, N), jnp.float32), xs, reverse=reverse)
    return jnp.moveaxis(ys, 0, 1)


def rwkv_branch(cu, zc, mu, w0, w2, a0, a2, kk_s, ka, rk, lnx_g, lnx_b):
    B, S, _ = cu.shape
    f32 = jnp.float32
    u = neighbour_shift(cu, mu)
    r, k, v, lw, la = split_cols(u, (D_C, D_C, D_C, 2 * LORA_W, 2 * LORA_A))
    lw = lw.reshape(B, S, 2, LORA_W)
    la = la.reshape(B, S, 2, LORA_A)
    w_log = -jax.nn.softplus(-(w0 + jnp.einsum('bsdr,drc->bsdc', jnp.tanh(lw), w2)).astype(f32)) - 0.5
    decay = jnp.exp(-jnp.exp(w_log))
    a = jax.nn.sigmoid((a0 + jnp.einsum('bsdr,drc->bsdc', la, a2)).astype(f32))
    heads = lambda t: t.reshape(t.shape[:-1] + (H_C, HEAD_C))
    kk = heads((k * kk_s).astype(f32))
    kk = kk / jnp.maximum(jnp.sqrt(jnp.sum(kk * kk, axis=-1, keepdims=True)), 1e-12)
    k_dir = heads(k[:, :, None, :].astype(f32) * (1.0 + (a - 1.0) * ka))
    a_h, dec_h = heads(a), heads(decay)
    r_h, v_h = heads(r), heads(v)
    y_fwd = rwkv7_scan(r_h, dec_h[:, :, 0], k_dir[:, :, 0], v_h, kk, a_h[:, :, 0], reverse=False)
    y_bwd = rwkv7_scan(r_h, dec_h[:, :, 1], k_dir[:, :, 1], v_h, kk, a_h[:, :, 1], reverse=True)
    y = head_norm(y_fwd + y_bwd, LNX_EPS) * heads(lnx_g) + heads(lnx_b)
    k_bonus = 0.5 * (k_dir[:, :, 0] + k_dir[:, :, 1])
    bonus = jnp.sum(r_h * k_bonus * rk, axis=-1, keepdims=True) * v_h
    y = (y + bonus).reshape(B, S, D_C)
    return (y * jax.nn.silu(zc)).astype(cu.dtype)


def setup_inputs(seed: int = 0) -> dict:
    key = jax.random.key(seed)
    ks = iter(jax.random.split(key, 64))
    nrm = lambda shape, s: s * jax.random.normal(next(ks), shape, jnp.float32)
    uni = lambda shape, lo, hi: jax.random.uniform(next(ks), shape, jnp.float32, lo, hi)
    Dp = DEPTH
    D = D_MODEL
    hy_dmax = -math.log(HY_TARGET) / HY_FAST_PCT
    hy_dmin = -math.log(HY_TARGET) / HY_SLOW_PCT
    return {
        'x': nrm((BATCH, SEQ, D), 1.0),
        'c': nrm((BATCH, D), 1.0),
        'ada_w': nrm((Dp, D, 3 * D), 0.5 * D ** -0.5),
        'ada_b': nrm((Dp, 3 * D), 0.01),
        'pre_g': 1.0 + nrm((Dp, D), 0.05),
        'post_g': 1.0 + nrm((Dp, D), 0.05),
        'w_in': nrm((Dp, D, N_IN), D ** -0.5),
        'ml_conv_w': nrm((Dp, SHORT_CONV, D_A), SHORT_CONV ** -0.5),
        'ml_conv_b': nrm((Dp, D_A), 0.01),
        'ml_wq': nrm((Dp, H_A, DH_A, DH_A), DH_A ** -0.5),
        'ml_wk': nrm((Dp, H_A, DH_A, DH_A), DH_A ** -0.5),
        'ml_wv': nrm((Dp, H_A, DH_A, DH_A), DH_A ** -0.5),
        'ml_gate_w': nrm((Dp, 3 * D_A, 4 * H_A), 0.3 * (3 * D_A) ** -0.5),
        'ml_gate_b': jnp.concatenate([nrm((Dp, H_A), 0.1), uni((Dp, H_A), 3.0, 6.0),
                                      nrm((Dp, H_A), 0.1), uni((Dp, H_A), 3.0, 6.0)], axis=-1),
        'ml_norm_g': 1.0 + nrm((Dp, D_A), 0.05),
        'ml_skip': 1.0 + nrm((Dp, D_A), 0.05),
        'hy_conv_w': nrm((Dp, SHORT_CONV, 3 * D_B), SHORT_CONV ** -0.5),
        'hy_conv_b': nrm((Dp, 3 * D_B), 0.01),
        'hy_w1': nrm((Dp, HY_EMB, HY_HID), HY_EMB ** -0.5),
        'hy_b1': nrm((Dp, HY_HID), 0.1),
        'hy_w2': nrm((Dp, HY_HID, HY_HID), HY_HID ** -0.5),
        'hy_b2': nrm((Dp, HY_HID), 0.1),
        'hy_w3': nrm((Dp, HY_HID, HY_HID), HY_HID ** -0.5),
        'hy_b3': nrm((Dp, HY_HID), 0.1),
        'hy_freq': 1.0 + nrm((Dp, HY_HID), 0.1),
        'hy_w_out': nrm((Dp, HY_HID, 2 * D_B), 0.004),
        'hy_decay': uni((Dp, 2, D_B), hy_dmin, hy_dmax),
        'hy_bias': nrm((Dp, D_B), 1.0),
        'rw_mu': uni((Dp, N_SHIFT_C), 0.0, 1.0),
        'rw_w0': uni((Dp, 2, D_C), -6.5, -1.5),
        'rw_w2': nrm((Dp, 2, LORA_W, D_C), 0.1 * LORA_W ** -0.5),
        'rw_a0': nrm((Dp, 2, D_C), 0.1),
        'rw_a2': nrm((Dp, 2, LORA_A, D_C), 0.1 * LORA_A ** -0.5),
        'rw_kk': 0.85 + nrm((Dp, D_C), 0.05),
        'rw_ka': 1.0 + nrm((Dp, D_C), 0.05),
        'rw_rk': nrm((Dp, H_C, HEAD_C), 0.1),
        'rw_lnx_g': 1.0 + nrm((Dp, D_C), 0.05),
        'rw_lnx_b': nrm((Dp, D_C), 0.01),
        'w_branch_a': nrm((Dp, D_A, D), D_A ** -0.5),
        'w_branch_b': nrm((Dp, D_B, D), D_B ** -0.5),
        'w_branch_c': nrm((Dp, D_C, D), D_C ** -0.5),
        'w_out': nrm((Dp, D, D), D ** -0.5),
    }


def reference(x, c, ada_w, ada_b, pre_g, post_g, w_in,
              ml_conv_w, ml_conv_b, ml_wq, ml_wk, ml_wv, ml_gate_w, ml_gate_b, ml_norm_g, ml_skip,
              hy_conv_w, hy_conv_b, hy_w1, hy_b1, hy_w2, hy_b2, hy_w3, hy_b3, hy_freq, hy_w_out,
              hy_decay, hy_bias,
              rw_mu, rw_w0, rw_w2, rw_a0, rw_a2, rw_kk, rw_ka, rw_rk, rw_lnx_g, rw_lnx_b,
              w_branch_a, w_branch_b, w_branch_c, w_out):
    B, S, D = x.shape
    feats, t = hyena_positional_features(S)
    cond = jax.nn.silu(c)
    for l in range(DEPTH):
        mod = cond @ ada_w[l] + ada_b[l]
        shift, scale, gate = jnp.split(mod[:, None, :], 3, axis=-1)
        h = rmsnorm(x, pre_g[l]) * (1.0 + scale) + shift
        proj = h @ w_in[l]
        a_x, a_z, b_u, b_z, c_u, c_z, g_all = split_cols(
            proj, (D_A, D_A, 3 * D_B, D_B, N_SHIFT_C, D_C, 3 * D_MODEL))
        y_a = mlstm_branch(a_x, a_z, ml_conv_w[l], ml_conv_b[l], ml_wq[l], ml_wk[l], ml_wv[l],
                           ml_gate_w[l], ml_gate_b[l], ml_norm_g[l], ml_skip[l])
        filt = hyena_filter(feats, t, hy_w1[l], hy_b1[l], hy_w2[l], hy_b2[l], hy_w3[l], hy_b3[l],
                            hy_freq[l], hy_w_out[l], hy_decay[l])
        y_b = hyena_branch(b_u, b_z, hy_conv_w[l], hy_conv_b[l], filt, hy_bias[l])
        y_c = rwkv_branch(c_u, c_z, rw_mu[l], rw_w0[l], rw_w2[l], rw_a0[l], rw_a2[l], rw_kk[l],
                          rw_ka[l], rw_rk[l], rw_lnx_g[l], rw_lnx_b[l])
        g_a, g_b, g_c = jnp.split(jax.nn.sigmoid(g_all), 3, axis=-1)
        merged = (g_a * (y_a @ w_branch_a[l]) + g_b * (y_b @ w_branch_b[l])
                  + g_c * (y_c @ w_branch_c[l]))
        out = merged @ w_out[l]
        x = x + gate * rmsnorm(out, post_g[l])
    return x
```

```python
import math
from contextlib import ExitStack
import numpy as np
import ml_dtypes
import concourse.bass as bass
import concourse.mybir as mybir
from concourse.bass_utils import run_bass_kernel_spmd

AF = mybir.ActivationFunctionType
ALU = mybir.AluOpType
AX = mybir.AxisListType
F32 = mybir.dt.float32
BF16 = mybir.dt.bfloat16

S = 4096
D = 1024
DEPTH = 4
N_IN = 10496
NCORES = 8
EPOCH = 1000000


class KB:
    def __init__(self, nc):
        self.nc = nc
        self.eng = {"pe": nc.tensor, "act": nc.scalar, "dve": nc.vector, "pool": nc.gpsimd, "sp": nc.sync}
        self.cnt = {e: 0 for e in self.eng}
        self.sems = {}
        self.waited = {e: {} for e in self.eng}
        self.dq = {q: {"n": 0} for q in ("sp", "act", "pool")}
        self.R = 6
        self.trk = {}
        self.n_inst = 0
        self.n_wait = 0
        self.pe_strict = False

    def _sem(self, key):
        s = self.sems.get(key)
        if s is None:
            s = self.nc.alloc_semaphore("s_" + "_".join(str(k) for k in key))
            self.sems[key] = s
        return s

    @staticmethod
    def _k(v):
        if isinstance(v, tuple):
            ap, sub = v
            return ap, (ap.tensor.name, sub)
        return v, (v.tensor.name, None)

    def _deps(self, reads, writes):
        deps = {}

        def add(tok):
            if tok is None:
                return
            sk, val = tok
            if deps.get(sk, 0) < val:
                deps[sk] = val

        for name, sub in reads:
            ent = self.trk.get(name)
            if not ent:
                continue
            subs = [sub, None] if sub is not None else list(ent.keys())
            for s_ in subs:
                e = ent.get(s_)
                if e:
                    add(e["w"])
        for name, sub in writes:
            ent = self.trk.get(name)
            if not ent:
                continue
            subs = [sub, None] if sub is not None else list(ent.keys())
            for s_ in subs:
                e = ent.get(s_)
                if e:
                    add(e["w"])
                    for t in e["r"]:
                        add(t)
        return deps

    def _update(self, tok, reads, writes):
        for name, sub in reads:
            ent = self.trk.setdefault(name, {})
            e = ent.setdefault(sub, {"w": None, "r": []})
            e["r"].append(tok)
            if len(e["r"]) > 24:
                best = {}
                for sk, val in e["r"]:
                    if best.get(sk, 0) < val:
                        best[sk] = val
                e["r"] = list(best.items())
        for name, sub in writes:
            ent = self.trk.setdefault(name, {})
            if sub is None:
                ent.clear()
            ent[sub] = {"w": tok, "r": []}

    def _emit_waits(self, eng, deps, skip_self=False):
        e = self.eng[eng]
        w = self.waited[eng]
        for sk, val in deps.items():
            if skip_self and sk[0] == eng:
                continue
            if w.get(sk, 0) < val:
                e.wait_ge(self._sem(sk), val)
                w[sk] = val
                self.n_wait += 1

    def op(self, eng, fn, r=(), w=()):
        reads = [self._k(v)[1] for v in r]
        writes = [self._k(v)[1] for v in w]
        writes = writes + [k for k in reads if k[0].startswith("ps")]
        deps = self._deps(reads, writes)
        self._emit_waits(eng, deps, skip_self=(eng == "pe" and not self.pe_strict))
        self.cnt[eng] += 1
        c = self.cnt[eng]
        sk = (eng, c // EPOCH)
        val = c % EPOCH
        if val == 0:
            sk = (eng, c // EPOCH - 1)
            val = EPOCH
        ins = fn(self.eng[eng])
        ins.then_inc(self._sem(sk), 1)
        self.n_inst += 1
        self._update((sk, val), reads, writes)

    def dma(self, q, out, in_, **kw):
        o_ap, o_k = self._k(out)
        i_ap, i_k = self._k(in_)
        deps = self._deps([i_k], [o_k])
        st = self.dq[q]
        n = st["n"]
        st["n"] += 1
        slot = n % self.R
        use = n // self.R
        ep = use // 700
        sk = ("d" + q, slot, ep)
        val = 16 * (use % 700 + 1)
        if use % 700 > 0:
            deps_prev = {sk: val - 16}
        elif use > 0:
            deps_prev = {("d" + q, slot, ep - 1): 16 * 700}
        else:
            deps_prev = {}
        for k_, v_ in deps_prev.items():
            if deps.get(k_, 0) < v_:
                deps[k_] = v_
        self._emit_waits(q, deps)
        ins = self.eng[q].dma_start(out=o_ap, in_=i_ap, **kw)
        ins.then_inc(self._sem(sk), 16)
        self.n_inst += 1
        self._update((sk, val), [i_k], [o_k])

    def barrier(self):
        deps = {}
        for e, c in self.cnt.items():
            if c == 0:
                continue
            ep, val = c // EPOCH, c % EPOCH
            if val == 0:
                ep, val = ep - 1, EPOCH
            deps[(e, ep)] = val
        for q, st in self.dq.items():
            n = st["n"]
            for slot in range(min(n, self.R)):
                last = ((n - 1 - slot) // self.R) * self.R + slot
                use = last // self.R
                deps[("d" + q, slot, use // 700)] = 16 * (use % 700 + 1)
        for e in self.eng:
            self._emit_waits(e, dict(deps))
        self.trk.clear()

    def wait_all(self, eng, vals):
        reads = [self._k(v)[1] for v in vals]
        deps = self._deps([], reads)
        self._emit_waits(eng, deps)

    def mm(self, out, lhsT, rhs, start=True, stop=True):
        o, l, r_ = self._k(out)[0], self._k(lhsT)[0], self._k(rhs)[0]
        self.op("pe", lambda e: e.matmul(o, l, r_, start=start, stop=stop), r=[lhsT, rhs], w=[out])

    def transpose(self, out, in_, ident):
        o, i, d = self._k(out)[0], self._k(in_)[0], self._k(ident)[0]
        self.op("pe", lambda e: e.transpose(o, i, d), r=[in_, ident], w=[out])

    def tt(self, eng, out, in0, in1, op):
        o, a, b = self._k(out)[0], self._k(in0)[0], self._k(in1)[0]
        self.op(eng, lambda e: e.tensor_tensor(out=o, in0=a, in1=b, op=op), r=[in0, in1], w=[out])

    def ts(self, eng, out, in0, s1, s2=None, op0=ALU.mult, op1=None, extra_r=()):
        o, a = self._k(out)[0], self._k(in0)[0]
        rr = [in0] + list(extra_r)
        s1v = s1
        s2v = s2
        if not isinstance(s1, (int, float)):
            rr.append(s1)
            s1v = self._k(s1)[0]
        if s2 is not None and not isinstance(s2, (int, float)):
            rr.append(s2)
            s2v = self._k(s2)[0]
        if op1 is None:
            self.op(eng, lambda e: e.tensor_scalar(out=o, in0=a, scalar1=s1v, scalar2=None, op0=op0), r=rr, w=[out])
        else:
            self.op(eng, lambda e: e.tensor_scalar(out=o, in0=a, scalar1=s1v, scalar2=s2v, op0=op0, op1=op1), r=rr, w=[out])

    def stt(self, eng, out, in0, scalar, in1, op0, op1):
        o, a, b = self._k(out)[0], self._k(in0)[0], self._k(in1)[0]
        rr = [in0, in1]
        sv = scalar
        if not isinstance(scalar, (int, float)):
            rr.append(scalar)
            sv = self._k(scalar)[0]
        self.op(eng, lambda e: e.scalar_tensor_tensor(out=o, in0=a, scalar=sv, in1=b, op0=op0, op1=op1), r=rr, w=[out])

    def act(self, out, in_, func, bias=None, scale=None):
        o, a = self._k(out)[0], self._k(in_)[0]
        rr = [in_]
        kw = {}
        if bias is not None:
            if isinstance(bias, (int, float)):
                kw["bias"] = float(bias)
            else:
                rr.append(bias)
                kw["bias"] = self._k(bias)[0]
        if scale is not None:
            if isinstance(scale, (int, float)):
                kw["scale"] = float(scale)
            else:
                rr.append(scale)
                kw["scale"] = self._k(scale)[0]
        self.op("act", lambda e: e.activation(out=o, in_=a, func=func, **kw), r=rr, w=[out])

    def rstd(self, out, in_, scale, eps):
        self.ts("dve", out, in_, scale, eps, op0=ALU.mult, op1=ALU.add)
        self.act(out, out, AF.Sqrt)
        o = self._k(out)[0]
        self.op("dve", lambda e: e.reciprocal(out=o, in_=o), r=[out], w=[out])

    def copy(self, eng, out, in_):
        o, a = self._k(out)[0], self._k(in_)[0]
        if eng == "act":
            self.op("act", lambda e: e.copy(out=o, in_=a), r=[in_], w=[out])
        else:
            self.op(eng, lambda e: e.tensor_copy(out=o, in_=a), r=[in_], w=[out])

    def memset(self, eng, out, val):
        o = self._k(out)[0]
        self.op(eng, lambda e: e.memset(o, val), r=[], w=[out])


ROW_AX, ROW_AZ, ROW_BU, ROW_BZ, ROW_CU, ROW_CZ, ROW_G = 0, 512, 1024, 2560, 3072, 6400, 7424


def fm(v, n):
    return np.ascontiguousarray(np.asarray(v, np.float32).reshape(n, 128).T)


class Prog:
    def __init__(self, n_layers=DEPTH, debug=False, stages=("mod", "inproj", "final")):
        self.nc = bass.Bass("TRN2", target_bir_lowering=False)
        self.kb = KB(self.nc)
        self.n_layers = n_layers
        self.debug = debug
        self.stages = stages
        self.inputs = {}
        self.din = {}

    def inp(self, name, arr):
        arr = np.ascontiguousarray(arr)
        self.inputs[name] = arr
        dt = {np.dtype(np.float32): F32, np.dtype(ml_dtypes.bfloat16): BF16}[arr.dtype]
        t = self.nc.dram_tensor(name, list(arr.shape), dt, kind="ExternalInput")
        self.din[name] = t
        return t

    def sb(self, name, shape, dt):
        self._uid = getattr(self, "_uid", 0) + 1
        return self.nc.sbuf_tensor("%s_u%d" % (name, self._uid), shape, dt)

    def scratch(self, name, shape, dt, dbg=False):
        kind = "ExternalOutput" if (dbg and self.debug) else "Internal"
        return self.nc.dram_tensor(name, list(shape), dt, kind=kind)

    def dump(self, name, ap, shape, dt):
        if not self.debug:
            return
        t = self.nc.dram_tensor("dbg_" + name, list(shape), dt, kind="ExternalOutput")
        self.kb.dma("sp", t.ap(), ap)

    def setup_common(self, P):
        nc, kb = self.nc, self.kb
        L = self.n_layers
        self.xin = self.inp("xT", P["xT"])
        self.cfm = self.inp("c_fm", P["c_fm"])
        self.ada_w = self.inp("ada_w", P["ada_w"])
        self.ada_b = self.inp("ada_b_fm", P["ada_b_fm"])
        self.pre_g = self.inp("pre_g_fm", P["pre_g_fm"])
        self.post_g = self.inp("post_g_fm", P["post_g_fm"])
        self.w_in = self.inp("w_in", P["w_in"])
        self.w_ba = self.inp("w_branch_a", P["w_branch_a"])
        self.w_bb = self.inp("w_branch_b", P["w_branch_b"])
        self.w_bc = self.inp("w_branch_c", P["w_branch_c"])
        self.w_out = self.inp("w_out", P["w_out"])
        self.c_ones = self.inp("c_ones_bf", np.ones((128, 128), ml_dtypes.bfloat16))
        self.xout = nc.dram_tensor("outT", [D, S], F32, kind="ExternalOutput")
        self.projT = self.scratch("projT", [N_IN, S], F32, dbg=True)
        if "yaT_in" in P:
            self.yaT, self.ybT, self.ycT = self.inp("yaT_in", P["yaT_in"]), self.inp("ybT_in", P["ybT_in"]), self.inp("ycT_in", P["ycT_in"])
        else:
            self.yaT = self.scratch("yaT", [512, S], BF16, dbg=True)
            self.ybT = self.scratch("ybT", [512, S], BF16, dbg=True)
            self.ycT = self.scratch("ycT", [1024, S], BF16, dbg=True)
        self.ones_bf = nc.alloc_sbuf_tensor("ones_bf", [128, 128], BF16)
        kb.dma("sp", self.ones_bf[:, :], self.c_ones.ap())
        self.cond = nc.alloc_sbuf_tensor("cond", [128, 8], F32)
        kb.dma("sp", self.cond[:, :], self.cfm.ap())
        kb.act(self.cond[:, :], self.cond[:, :], AF.Silu)
        self.modA = nc.alloc_sbuf_tensor("modA", [128, 8], F32)
        self.modB = nc.alloc_sbuf_tensor("modB", [128, 8], F32)
        self.modG = nc.alloc_sbuf_tensor("modG", [128, 8], F32)
        self.ps = [nc.alloc_psum_tensor("ps%d" % i, [128, 512], F32) for i in range(8)]

    def stage_mod(self, l):
        nc, kb = self.nc, self.kb
        with self.sb("mod_w0", [128, 8, 512], F32) as w0, self.sb("mod_w1", [128, 8, 512], F32) as w1, \
                self.sb("mod_t", [128, 24], F32) as mt, self.sb("mod_p", [128, 8], F32) as mp:
            wb = [w0, w1]
            ps = self.ps[0]
            wv = self.ada_w.ap()[l].rearrange("(kc p) n -> p kc n", p=128)
            for cb in range(6):
                wt = wb[cb % 2]
                kb.dma("sp", wt[:, :, :], wv[:, :, cb * 512:(cb + 1) * 512])
                for j in range(4):
                    n = cb * 4 + j
                    for kc in range(8):
                        kb.mm(ps[:, n:n + 1], wt[:, kc, j * 128:(j + 1) * 128], self.cond[:, kc:kc + 1],
                              start=(kc == 0), stop=(kc == 7))
            kb.dma("pool", mt[:, :], self.ada_b.ap()[l])
            kb.tt("dve", mt[:, :], mt[:, :], ps[:, 0:24], ALU.add)
            kb.dma("pool", mp[:, :], self.pre_g.ap()[l])
            kb.stt("dve", self.modA[:, :], mt[:, 8:16], 1.0, mp[:, :], ALU.add, ALU.mult)
            kb.copy("dve", self.modB[:, :], mt[:, 0:8])
            kb.dma("pool", mp[:, :], self.post_g.ap()[l])
            kb.tt("dve", self.modG[:, :], mt[:, 16:24], mp[:, :], ALU.mult)
            self.dump("mt%d" % l, mt[:, :], [128, 24], F32)
            self.dump("modA%d" % l, self.modA[:, :], [128, 8], F32)

    def stage_inproj(self, l):
        nc, kb = self.nc, self.kb
        xsrc = self.xin if l == 0 else self.xout
        xv = xsrc.ap().rearrange("(kc p) t -> p kc t", p=128)
        with self.sb("hT", [128, 8, S], BF16) as hT:
            with self.sb("ip_x0", [128, 8, 512], F32) as x0, self.sb("ip_x1", [128, 8, 512], F32) as x1, \
                    self.sb("ip_sq", [128, 8, 512], BF16) as sq, self.sb("ip_rs", [128, 512], F32) as rs, \
                    self.sb("ip_tmp", [128, 2, 512], F32) as tmp:
                xb = [x0, x1]
                for tt in range(8):
                    xt = xb[tt % 2]
                    tsl = slice(tt * 512, (tt + 1) * 512)
                    kb.dma("sp", xt[:, :, :], (xv[:, :, tsl], tt))
                    ps = self.ps[tt % 2]
                    for kc in range(8):
                        kb.act((sq[:, kc, :], kc), xt[:, kc, :], AF.Square)
                        kb.mm(ps[:, :], self.ones_bf[:, :], (sq[:, kc, :], kc), start=(kc == 0), stop=(kc == 7))
                    kb.rstd(rs[:, :], ps[:, :], 1.0 / D, 1e-6)
                    for kc in range(8):
                        tm = (tmp[:, kc % 2, :], kc % 2)
                        kb.stt("dve", tm, xt[:, kc, :], self.modA[:, kc:kc + 1], rs[:, :], ALU.mult, ALU.mult)
                        kb.act((hT[:, kc, tsl], (kc, tt)), tm, AF.Identity, bias=self.modB[:, kc:kc + 1])
            kb.barrier()
            self.dump("hT%d" % l, hT[:, :, :], [128, 8, S], BF16)
            with self.sb("ip_wf0", [128, 8, 512], F32) as wf0, self.sb("ip_wf1", [128, 8, 512], F32) as wf1, \
                    self.sb("ip_wb0", [128, 8, 512], BF16) as wb0, self.sb("ip_wb1", [128, 8, 512], BF16) as wb1, \
                    self.sb("ip_o0", [128, 512], F32) as o0, self.sb("ip_o1", [128, 512], F32) as o1, \
                    self.sb("ip_o2", [128, 512], F32) as o2, self.sb("ip_o3", [128, 512], F32) as o3:
                wf, wb, ob = [wf0, wf1], [wb0, wb1], [o0, o1, o2, o3]
                wv = self.w_in.ap()[l].rearrange("(kc p) n -> p kc n", p=128)
                blocks = [(c0, min(512, N_IN - c0)) for c0 in range(0, N_IN, 512)]
                cnt = 0

                def load(bi):
                    c0, w = blocks[bi]
                    kb.dma("sp", wf[bi % 2][:, :, 0:w], wv[:, :, c0:c0 + w])
                    for kc in range(8):
                        if kc % 2 == 0:
                            kb.copy("pool", wb[bi % 2][:, kc, 0:w], wf[bi % 2][:, kc, 0:w])
                        else:
                            kb.copy("dve", wb[bi % 2][:, kc, 0:w], wf[bi % 2][:, kc, 0:w])

                load(0)
                for bi, (c0, w) in enumerate(blocks):
                    if bi + 1 < len(blocks):
                        load(bi + 1)
                    wt = wb[bi % 2]
                    for tt in range(8):
                        tsl = slice(tt * 512, (tt + 1) * 512)
                        for oc in range(w // 128):
                            row0 = c0 + oc * 128
                            ps = self.ps[2 + cnt % 6]
                            for kc in range(8):
                                kb.mm(ps[:, :], wt[:, kc, oc * 128:(oc + 1) * 128], (hT[:, kc, tsl], (kc, tt)),
                                      start=(kc == 0), stop=(kc == 7))
                            ot = ob[cnt % 4]
                            if (ROW_AZ <= row0 < ROW_BU) or (ROW_BZ <= row0 < ROW_CU) or (ROW_CZ <= row0 < ROW_G):
                                kb.act(ot[:, :], ps[:, :], AF.Silu)
                            elif row0 >= ROW_G:
                                kb.act(ot[:, :], ps[:, :], AF.Sigmoid)
                            elif cnt % 2 == 0:
                                kb.copy("dve", ot[:, :], ps[:, :])
                            else:
                                kb.copy("act", ot[:, :], ps[:, :])
                            kb.dma("pool" if cnt % 2 == 0 else "act", (self.projT.ap()[row0:row0 + 128, tsl], (row0 // 128, tt)), ot[:, :])
                            cnt += 1

    def stage_final(self, l):
        nc, kb = self.nc, self.kb
        import os
        LQ = os.environ.get('LQ', 'pool')
        FS = os.environ.get('FSTEPS', 'WLMOS')
        xsrc = self.xin if l == 0 else self.xout
        xv = xsrc.ap().rearrange("(kc p) t -> p kc t", p=128)
        ov = self.xout.ap().rearrange("(kc p) t -> p kc t", p=128)
        with self.sb("fn_wa", [128, 4, D], BF16) as wa, self.sb("fn_wb", [128, 4, D], BF16) as wbb, \
                self.sb("fn_wc", [128, 8, D], BF16) as wc, self.sb("fn_wo", [128, 8, D], BF16) as wo, \
                self.sb("fn_stage", [128, 4, D], F32) as stg, \
                self.sb("fn_ya", [128, 4, 512], BF16) as ya, self.sb("fn_yb", [128, 4, 512], BF16) as yb, \
                self.sb("fn_yc", [128, 8, 512], BF16) as yc, self.sb("fn_g", [128, 3, 512], F32) as g, \
                self.sb("fn_t", [128, 3, 512], F32) as t3, self.sb("fn_m", [128, 8, 512], BF16) as mg, \
                self.sb("fn_o", [128, 8, 512], F32) as of, self.sb("fn_sq", [128, 512], BF16) as sq, \
                self.sb("fn_rs", [128, 512], F32) as rs, self.sb("fn_x", [128, 8, 512], F32) as xt:
            for (src, dst, nk) in () if 'W' not in FS else ((self.w_ba, wa, 4), (self.w_bb, wbb, 4), (self.w_bc, wc, 8), (self.w_out, wo, 8)):
                sv = src.ap()[l].rearrange("(kc p) n -> p kc n", p=128)
                for k0 in range(0, nk, 4):
                    kb.dma("sp", stg[:, :, :], sv[:, k0:k0 + 4, :])
                    for j in range(4):
                        kb.copy("dve" if j % 2 == 0 else "pool", dst[:, k0 + j, :], stg[:, j, :])
            yav = self.yaT.ap().rearrange("(kc p) t -> p kc t", p=128)
            ybv = self.ybT.ap().rearrange("(kc p) t -> p kc t", p=128)
            ycv = self.ycT.ap().rearrange("(kc p) t -> p kc t", p=128)
            for tt in range(8 if 'L' in FS else 0):
                tsl = slice(tt * 512, (tt + 1) * 512)
                kb.dma("sp", ya[:, :, :], yav[:, :, tsl])
                kb.dma("sp", yb[:, :, :], ybv[:, :, tsl])
                kb.dma("sp", yc[:, :, :], ycv[:, :, tsl])
                kb.dma(LQ, xt[:, :, :], (xv[:, :, tsl], tt))
                for oc in range(8 if 'M' in FS else 0):
                    osl = slice(oc * 128, (oc + 1) * 128)
                    pa, pb, pc = self.ps[0 + 3 * (oc % 2)], self.ps[1 + 3 * (oc % 2)], self.ps[2 + 3 * (oc % 2)]
                    for kc in range(4):
                        kb.mm(pa[:, :], wa[:, kc, osl], ya[:, kc, :], start=(kc == 0), stop=(kc == 3))
                    for kc in range(4):
                        kb.mm(pb[:, :], wbb[:, kc, osl], yb[:, kc, :], start=(kc == 0), stop=(kc == 3))
                    for kc in range(8):
                        kb.mm(pc[:, :], wc[:, kc, osl], yc[:, kc, :], start=(kc == 0), stop=(kc == 7))
                    for br in range(3):
                        r0 = ROW_G + br * 1024 + oc * 128
                        kb.dma("sp" if br != 1 else LQ, (g[:, br, :], br), self.projT.ap()[r0:r0 + 128, tsl])
                    kb.tt("dve", t3[:, 0, :], pa[:, :], (g[:, 0, :], 0), ALU.mult)
                    kb.tt("dve", t3[:, 1, :], pb[:, :], (g[:, 1, :], 1), ALU.mult)
                    kb.tt("dve", t3[:, 2, :], pc[:, :], (g[:, 2, :], 2), ALU.mult)
                    kb.tt("pool", t3[:, 0, :], t3[:, 0, :], t3[:, 1, :], ALU.add)
                    kb.tt("pool", (mg[:, oc, :], oc), t3[:, 0, :], t3[:, 2, :], ALU.add)
                if 'O' not in FS:
                    continue
                pss = self.ps[6]
                for oc in range(8):
                    osl = slice(oc * 128, (oc + 1) * 128)
                    po = self.ps[oc % 2]
                    for kc in range(8):
                        kb.mm(po[:, :], wo[:, kc, osl], (mg[:, kc, :], kc), start=(kc == 0), stop=(kc == 7))
                    kb.copy("dve", (of[:, oc, :], oc), po[:, :])
                    kb.act(sq[:, :], po[:, :], AF.Square)
                    kb.mm(pss[:, :], self.ones_bf[:, :], sq[:, :], start=(oc == 0), stop=(oc == 7))
                kb.rstd(rs[:, :], pss[:, :], 1.0 / D, 1e-6)
                for oc in range(8):
                    kb.stt("dve", (of[:, oc, :], oc), (of[:, oc, :], oc), self.modG[:, oc:oc + 1], rs[:, :], ALU.mult, ALU.mult)
                    kb.tt("pool", (xt[:, oc, :], oc), (xt[:, oc, :], oc), (of[:, oc, :], oc), ALU.add)
                if 'S' in FS:
                    kb.dma("pool", (ov[:, :, tsl], tt), xt[:, :, :])

    def finish(self):
        kb = self.kb
        kb.wait_all("sp", [self.xout.ap()])


def host_inputs(inp, b, n_layers=DEPTH, l0=0):
    L = n_layers
    sl = slice(l0, l0 + L)
    f32 = np.float32
    P = {}
    P["xT"] = np.ascontiguousarray(np.asarray(inp["x"][b], f32).T)
    P["c_fm"] = fm(inp["c"][b], 8)
    P["ada_w"] = np.asarray(inp["ada_w"][sl], f32)
    P["ada_b_fm"] = np.stack([fm(inp["ada_b"][l], 24) for l in range(l0, l0 + L)])
    P["pre_g_fm"] = np.stack([fm(inp["pre_g"][l], 8) for l in range(l0, l0 + L)])
    P["post_g_fm"] = np.stack([fm(inp["post_g"][l], 8) for l in range(l0, l0 + L)])
    P["w_in"] = np.asarray(inp["w_in"][sl], f32)
    for k in ("w_branch_a", "w_branch_b", "w_branch_c", "w_out"):
        P[k] = np.asarray(inp[k][sl], f32)
    mlstm_host(inp, P, l0, L)
    hyena_host(inp, P, l0, L)
    rwkv_host(inp, P, l0, L)
    return P


def bc_last(ap, n):
    pat = [list(p) for p in ap.ap]
    return bass.AP(ap.tensor, ap.offset, pat + [[0, n]])


def row_bcast(t, off, n, parts=128):
    return bass.AP(t, off, [[0, parts], [1, n]])


def mlstm_host(inp, P, l0, L):
    f32 = np.float32
    ls = range(l0, l0 + L)
    P["ml_cw"] = np.stack([np.asarray(inp["ml_conv_w"][l], f32).reshape(3, 4, 128).transpose(2, 1, 0) for l in ls])
    P["ml_cb"] = np.stack([fm(inp["ml_conv_b"][l], 4) for l in ls])
    P["ml_wqkv"] = np.stack([np.stack([np.asarray(inp[k][l], f32).transpose(1, 0, 2) for k in ("ml_wq", "ml_wk", "ml_wv")], axis=1)
                             for l in ls])
    perm = [d * 8 + 0 * 4 + h for d in range(2) for h in range(4)] + [d * 8 + 4 + h for d in range(2) for h in range(4)]
    P["ml_gw"] = np.stack([np.asarray(inp["ml_gate_w"][l], f32)[:, perm].reshape(12, 128, 16).transpose(1, 0, 2) for l in ls])
    gb = np.stack([np.asarray(inp["ml_gate_b"][l], f32)[perm] for l in ls])
    P["ml_gbi"] = np.ascontiguousarray(gb[:, 0:8, None])
    P["ml_gbf"] = np.ascontiguousarray(gb[:, 8:16, None])
    P["ml_ng"] = np.stack([fm(inp["ml_norm_g"][l], 4) for l in ls])
    P["ml_sk"] = np.stack([fm(inp["ml_skip"][l], 4) for l in ls])
    s_ = np.arange(128)
    P["c_maskf"] = (s_[:, None] <= s_[None, :]).astype(f32)
    P["c_maskb"] = (s_[:, None] >= s_[None, :]).astype(f32)
    P["c_ident"] = np.eye(128, dtype=f32)
    rst = np.ones((8, S), f32)
    rst[:, 0::128] = 0.0
    P["c_rst"] = rst


def _ml_setup(self, P):
    nc, kb = self.nc, self.kb
    for k in ("ml_cw", "ml_cb", "ml_wqkv", "ml_gw", "ml_gbi", "ml_gbf", "ml_ng", "ml_sk", "c_maskf", "c_maskb", "c_ident", "c_rst"):
        setattr(self, k, self.inp(k, P[k]))
    self.xcT = self.scratch("xcT", [512, S], F32)
    self.gsc = self.scratch("gsc", [4, 8, S], F32, dbg=True)
    self.dgs = self.scratch("dgs", [8, 32], F32)
    self.maskf = nc.alloc_sbuf_tensor("maskf", [128, 128], F32)
    self.maskb = nc.alloc_sbuf_tensor("maskb", [128, 128], F32)
    self.ident = nc.alloc_sbuf_tensor("ident", [128, 128], F32)
    kb.dma("sp", self.maskf[:, :], self.c_maskf.ap())
    kb.dma("sp", self.maskb[:, :], self.c_maskb.ap())
    kb.dma("sp", self.ident[:, :], self.c_ident.ap())


def _stage_mlstm(self, l):
    nc, kb = self.nc, self.kb
    C0 = -0.5 * math.log(128.0)
    ps = self.ps
    with self.sb("ml_wst", [128, 3, 4, 128], F32) as wst, self.sb("ml_wqb", [128, 3, 4, 128], BF16) as wqb, \
            self.sb("ml_cw", [128, 4, 3], F32) as cw, self.sb("ml_cb", [128, 4], F32) as cb, \
            self.sb("ml_ng", [128, 4], F32) as ng, self.sb("ml_sk", [128, 4], F32) as sk, \
            self.sb("ml_ekh", [128, 2, 32, 8], F32) as ekh, self.sb("ml_dgb", [128, 8, 32], F32) as dgb:
        kb.dma("sp", wst[:, :, :, :], self.ml_wqkv.ap()[l])
        kb.copy("dve", wqb[:, :, :, :], wst[:, :, :, :])
        kb.dma("sp", cw[:, :, :], self.ml_cw.ap()[l])
        kb.dma("sp", cb[:, :], self.ml_cb.ap()[l])
        kb.dma("sp", ng[:, :], self.ml_ng.ap()[l])
        kb.dma("sp", sk[:, :], self.ml_sk.ap()[l])
        with self.sb("ml_qkv", [128, 12, S], BF16) as qkv:
            with self.sb("ml_xa", [128, S + 2], F32) as xa, self.sb("ml_acc", [128, S], F32) as acc, \
                    self.sb("ml_xcb", [128, S], BF16) as xcb, self.sb("ml_xab", [128, S], BF16) as xab:
                for h in range(4):
                    kb.memset("pool", xa[:, 0:1], 0.0)
                    kb.memset("pool", xa[:, S + 1:S + 2], 0.0)
                    kb.dma("sp", xa[:, 1:S + 1], self.projT.ap()[ROW_AX + h * 128:ROW_AX + (h + 1) * 128, :])
                    kb.ts("dve", acc[:, :], xa[:, 0:S], cw[:, h, 0:1], op0=ALU.mult)
                    kb.stt("dve", acc[:, :], xa[:, 1:S + 1], cw[:, h, 1:2], acc[:, :], ALU.mult, ALU.add)
                    kb.stt("dve", acc[:, :], xa[:, 2:S + 2], cw[:, h, 2:3], acc[:, :], ALU.mult, ALU.add)
                    kb.act(acc[:, :], acc[:, :], AF.Silu, bias=cb[:, h:h + 1])
                    kb.dma("pool", self.xcT.ap()[h * 128:(h + 1) * 128, :], acc[:, :])
                    kb.copy("pool", xcb[:, :], acc[:, :])
                    kb.copy("pool", xab[:, :], xa[:, 1:S + 1])
                    for tt in range(8):
                        tsl = slice(tt * 512, (tt + 1) * 512)
                        for wi, src in ((0, xcb), (1, xcb), (2, xab)):
                            p_ = ps[(tt * 3 + wi) % 6]
                            kb.mm(p_[:, :], wqb[:, wi, h, :], src[:, tsl])
                            kb.copy("act" if wi == 1 else "dve", (qkv[:, wi * 4 + h, tsl], (wi * 4 + h, tt)), p_[:, :])
            kb.barrier()
            with ExitStack() as es_g:
                gwf = es_g.enter_context(self.sb("ml_gwf", [128, 12, 16], F32))
                gwb = es_g.enter_context(self.sb("ml_gwb", [128, 12, 16], BF16))
                gb = es_g.enter_context(self.sb("ml_gb", [8, 3], F32))
                rst = es_g.enter_context(self.sb("ml_rst", [8, S], F32))
                LI = es_g.enter_context(self.sb("ml_LI", [8, S], F32))
                NLF = es_g.enter_context(self.sb("ml_NLF", [8, S], F32))
                NBC = es_g.enter_context(self.sb("ml_NBC", [8, S], F32))
                T1 = es_g.enter_context(self.sb("ml_T1", [8, S], F32))
                T2 = es_g.enter_context(self.sb("ml_T2", [8, S], F32))
                sm = es_g.enter_context(self.sb("ml_sm", [8, 3, 32], F32))
                kb.dma("sp", gwf[:, :, :], self.ml_gw.ap()[l])
                kb.copy("dve", gwb[:, :, :], gwf[:, :, :])
                kb.dma("sp", gb[:, 0:1], self.ml_gbi.ap()[l])
                kb.dma("sp", gb[:, 1:2], self.ml_gbf.ap()[l])
                kb.ts("dve", gb[:, 2:3], gb[:, 1:2], -1.0, op0=ALU.mult)
                kb.dma("sp", rst[:, :], self.c_rst.ap())
                for tt in range(8):
                    tsl = slice(tt * 512, (tt + 1) * 512)
                    pi, pf = ps[6], ps[7]
                    for c in range(12):
                        kb.mm(pi[0:8, :], gwb[:, c, 0:8], (qkv[:, c, tsl], (c, tt)), start=(c == 0), stop=(c == 11))
                    for c in range(12):
                        kb.mm(pf[0:8, :], gwb[:, c, 8:16], (qkv[:, c, tsl], (c, tt)), start=(c == 0), stop=(c == 11))
                    kb.act(LI[:, tsl], pi[0:8, :], AF.Identity, bias=gb[:, 0:1])
                    kb.act(NLF[:, tsl], pf[0:8, :], AF.Exp, bias=gb[:, 2:3], scale=-1.0)
                kb.act(NLF[:, :], NLF[:, :], AF.Ln, bias=1.0)
                nlf, nbc, rs_ = NLF[:, :], NBC[:, :], rst[:, :]
                kb.op("dve", lambda e: e.tensor_tensor_scan(out=nbc, data0=rs_, data1=nlf, initial=0.0, op0=ALU.mult, op1=ALU.add),
                      r=[NLF[:, :], rst[:, :]], w=[NBC[:, :]])
                self.dump("LI%d" % l, LI[:, :], [8, S], F32)
                self.dump("NLF%d" % l, NLF[:, :], [8, S], F32)
                self.dump("NBC%d" % l, NBC[:, :], [8, S], F32)
                self.dump("qkv%d" % l, qkv[:, :, 0:512], [128, 12, 512], BF16)
                ngv = bass.AP(NBC, 127, [list(NBC[:, :].ap[0]), [128, 32]])
                kb.copy("dve", sm[:, 0, :], ngv)
                kb.act(sm[:, 1, :], sm[:, 0, :], AF.Exp, scale=-1.0)
                kb.act(sm[:, 2, :], sm[:, 0, :], AF.Exp)
                kb.dma("sp", self.dgs.ap(), sm[:, 1, :])
                v3 = lambda t_: t_[:, :].rearrange("p (j i) -> p j i", i=128)
                kb.act(T1[:, :], NBC[:, :], AF.Exp, scale=-1.0)
                kb.dma("sp", (self.gsc.ap()[0], 0), T1[:, :])
                kb.tt("dve", T2[:, :], LI[:, :], NBC[:, :], ALU.add)
                kb.act(T2[:, :], T2[:, :], AF.Exp, bias=C0)
                kb.dma("sp", (self.gsc.ap()[1], 1), T2[:, :])
                kb.tt("dve", v3(T1), v3(T2), bc_last(sm[:, 1, :], 128), ALU.mult)
                pt = ps[0]
                for j in range(32):
                    kb.transpose(pt[:, j * 8:(j + 1) * 8], T1[:, j * 128:(j + 1) * 128], self.ident[0:8, 0:8])
                kb.copy("dve", ekh[:, 0, :, :], pt[:, 0:256].rearrange("p (j r) -> p j r", r=8))
                kb.tt("dve", T2[:, :], NBC[:, :], NLF[:, :], ALU.subtract)
                kb.tt("dve", NLF[:, :], LI[:, :], T2[:, :], ALU.subtract)
                kb.act(T2[:, :], T2[:, :], AF.Exp)
                kb.tt("dve", v3(T2), v3(T2), bc_last(sm[:, 1, :], 128), ALU.mult)
                kb.dma("sp", (self.gsc.ap()[2], 2), T2[:, :])
                kb.act(T1[:, :], NLF[:, :], AF.Exp, bias=C0)
                pt = ps[1]
                for j in range(32):
                    kb.transpose(pt[:, j * 8:(j + 1) * 8], T1[:, j * 128:(j + 1) * 128], self.ident[0:8, 0:8])
                kb.copy("dve", ekh[:, 1, :, :], pt[:, 0:256].rearrange("p (j r) -> p j r", r=8))
                kb.tt("dve", v3(NBC), v3(T1), bc_last(sm[:, 2, :], 128), ALU.mult)
                kb.dma("sp", (self.gsc.ap()[3], 3), NBC[:, :])
                kb.dma("sp", dgb[:, :, :].rearrange("p a b -> p (a b)"), row_bcast(self.dgs, 0, 256))
        kb.barrier()
        with ExitStack() as es_b:
            stg = es_b.enter_context(self.sb("mb_stage", [128, S], F32))
            xcb = es_b.enter_context(self.sb("mb_xcb", [128, S], BF16))
            xab = es_b.enter_context(self.sb("mb_xab", [128, S], BF16))
            XS = es_b.enter_context(self.sb("mb_xs", [128, S], F32))
            qT = es_b.enter_context(self.sb("mb_qT", [128, S], BF16))
            kT = es_b.enter_context(self.sb("mb_kT", [128, S], BF16))
            ktm = es_b.enter_context(self.sb("mb_ktm", [128, 32, 128], BF16))
            vtm = es_b.enter_context(self.sb("mb_vtm", [128, 32, 129], BF16))
            EQ = es_b.enter_context(self.sb("mb_eq", [128, S], F32))
            EK = es_b.enter_context(self.sb("mb_ek", [128, S], F32))
            qs = es_b.enter_context(self.sb("mb_qs", [128, S], BF16))
            ks = es_b.enter_context(self.sb("mb_ks", [128, S], BF16))
            kh = es_b.enter_context(self.sb("mb_kh", [128, 32, 128], BF16))
            Cst = es_b.enter_context(self.sb("mb_C", [128, 129], F32))
            Cbf = es_b.enter_context(self.sb("mb_Cb", [128, 129], BF16))
            hs = es_b.enter_context(self.sb("mb_hs", [128, 32, 128], F32))
            STt = es_b.enter_context(self.sb("mb_st", [128, 2, 128], BF16))
            rd = es_b.enter_context(self.sb("mb_rd", [128, 2], F32))
            stat = es_b.enter_context(self.sb("mb_stat", [128, 4, 32], F32))
            yst = es_b.enter_context(self.sb("mb_y", [128, S], BF16))
            tmp = es_b.enter_context(self.sb("mb_tmp", [128, 512], F32))
            for h in range(4):
                kb.dma("sp", stg[:, :], self.xcT.ap()[h * 128:(h + 1) * 128, :])
                kb.copy("pool", xcb[:, :], stg[:, :])
                kb.ts("dve", XS[:, :], stg[:, :], sk[:, h:h + 1], op0=ALU.mult)
                kb.dma("sp", stg[:, :], self.projT.ap()[ROW_AX + h * 128:ROW_AX + (h + 1) * 128, :])
                kb.copy("pool", xab[:, :], stg[:, :])
                for tt in range(8):
                    tsl = slice(tt * 512, (tt + 1) * 512)
                    p0, p1 = ps[(2 * tt) % 6], ps[(2 * tt + 1) % 6]
                    kb.mm(p0[:, :], wqb[:, 0, h, :], xcb[:, tsl])
                    kb.copy("act", qT[:, tsl], p0[:, :])
                    kb.mm(p1[:, :], wqb[:, 1, h, :], xcb[:, tsl])
                    kb.copy("dve", kT[:, tsl], p1[:, :])
                kb.memset("pool", vtm[:, :, 128:129], 1.0)
                for g in range(8):
                    p0, p1 = ps[(2 * g) % 6], ps[(2 * g + 1) % 6]
                    for i in range(4):
                        j = g * 4 + i
                        kb.mm(p0[:, i * 128:(i + 1) * 128], xcb[:, j * 128:(j + 1) * 128], wqb[:, 1, h, :])
                    for i in range(4):
                        j = g * 4 + i
                        kb.mm(p1[:, i * 128:(i + 1) * 128], xab[:, j * 128:(j + 1) * 128], wqb[:, 2, h, :])
                    kb.copy("act", ktm[:, g * 4:(g + 1) * 4, :], p0[:, :].rearrange("p (i e) -> p i e", e=128))
                    kb.copy("dve", vtm[:, g * 4:(g + 1) * 4, 0:128], p1[:, :].rearrange("p (i e) -> p i e", e=128))
                for d in range(2):
                    r = d * 4 + h
                    kb.dma("sp", EQ[:, :], row_bcast(self.gsc, ((d * 2 + 0) * 8 + r) * S, S))
                    kb.dma("pool", EK[:, :], row_bcast(self.gsc, ((d * 2 + 1) * 8 + r) * S, S))
                    kb.tt("pool", qs[:, :], qT[:, :], EQ[:, :], ALU.mult)
                    kb.tt("dve", ks[:, :], kT[:, :], EK[:, :], ALU.mult)
                    kb.tt("pool", kh[:, :, :], ktm[:, :, :], bc_last(ekh[:, d, :, r], 128), ALU.mult)
                    mask = self.maskf if d == 0 else self.maskb
                    order = list(range(32)) if d == 0 else list(range(31, -1, -1))
                    for n_, j in enumerate(order):
                        first = (n_ == 0)
                        csl = slice(j * 128, (j + 1) * 128)
                        b3 = 3 * (n_ % 2)
                        p_s, p_o, p_c = ps[b3], ps[b3 + 1], ps[b3 + 2]
                        st_ = (STt[:, n_ % 2, :], n_ % 2)
                        kb.mm(p_s[:, 0:128], ks[:, csl], qs[:, csl])
                        kb.tt("dve", st_, p_s[:, 0:128], mask[:, :], ALU.mult)
                        kb.mm(p_o[:, 0:129], st_, vtm[:, j, :], start=True, stop=first)
                        if not first:
                            kb.mm(p_o[:, 0:129], qs[:, csl], Cbf[:, :], start=False, stop=True)
                        kb.mm(p_c[:, 0:129], kh[:, j, :], vtm[:, j, :])
                        if first:
                            kb.copy("dve", Cst[:, :], p_c[:, 0:129])
                        else:
                            kb.stt("dve", Cst[:, :], Cst[:, :], dgb[:, r, j:j + 1], p_c[:, 0:129], ALU.mult, ALU.add)
                        kb.copy("act", Cbf[:, :], Cst[:, :])
                        rdv = (rd[:, n_ % 2:n_ % 2 + 1], n_ % 2)
                        kb.act(rdv, p_o[:, 128:129], AF.Abs)
                        kb.ts("dve", rdv, rdv, 1.0, op0=ALU.max)
                        rdo = rd[:, n_ % 2:n_ % 2 + 1]
                        kb.op("dve", lambda e, o=rdo: e.reciprocal(out=o, in_=o), r=[rdv], w=[rdv])
                        if d == 0:
                            kb.act((hs[:, j, :], j), p_o[:, 0:128], AF.Identity, scale=rdv)
                        else:
                            kb.stt("dve", (hs[:, j, :], j), p_o[:, 0:128], rdv, (hs[:, j, :], j), ALU.mult, ALU.add)
                hs3 = hs[:, :, :]
                sq3 = EQ[:, :].rearrange("p (j i) -> p j i", i=128)
                kb.op("dve", lambda e: e.tensor_reduce(out=stat[:, 0, :], in_=hs3, axis=AX.X, op=ALU.add), r=[hs3], w=[stat[:, :, :]])
                kb.ts("dve", stat[:, 0, :], stat[:, 0, :], 1.0 / 128, op0=ALU.mult)
                kb.tt("dve", hs3, hs3, bc_last(stat[:, 0, :], 128), ALU.subtract)
                kb.act(sq3, hs3, AF.Square)
                kb.op("dve", lambda e: e.tensor_reduce(out=stat[:, 1, :], in_=sq3, axis=AX.X, op=ALU.add), r=[EQ[:, :]], w=[stat[:, :, :]])
                kb.rstd(stat[:, 1, :], stat[:, 1, :], 1.0 / 128, 1e-5)
                kb.tt("pool", hs3, hs3, bc_last(stat[:, 1, :], 128), ALU.mult)
                kb.dma("sp", stg[:, :], self.projT.ap()[ROW_AZ + h * 128:ROW_AZ + (h + 1) * 128, :])
                for g in range(8):
                    tsl = slice(g * 512, (g + 1) * 512)
                    pt = ps[6 + g % 2]
                    for i in range(4):
                        kb.transpose(pt[:, i * 128:(i + 1) * 128], hs[:, g * 4 + i, :], self.ident[:, :])
                    kb.stt("dve", tmp[:, :], pt[:, :], ng[:, h:h + 1], XS[:, tsl], ALU.mult, ALU.add)
                    kb.tt("pool", yst[:, tsl], tmp[:, :], stg[:, tsl], ALU.mult)
                kb.dma("pool", self.yaT.ap()[h * 128:(h + 1) * 128, :], yst[:, :])


Prog.ml_setup = _ml_setup
Prog.stage_mlstm = _stage_mlstm


NF = 4224


def hyena_host(inp, P, l0, L):
    f32 = np.float32
    ls = range(l0, l0 + L)
    P["hy_cw"] = np.stack([np.asarray(inp["hy_conv_w"][l], f32).reshape(3, 12, 128).transpose(2, 1, 0) for l in ls])
    P["hy_cb"] = np.stack([fm(inp["hy_conv_b"][l], 12) for l in ls])
    P["hy_bias"] = np.stack([fm(inp["hy_bias"][l], 4) for l in ls])
    P["hy_w1"] = np.asarray(inp["hy_w1"][l0:l0 + L], f32)
    P["hy_w23"] = np.stack([np.stack([inp["hy_w2"][l], inp["hy_w3"][l]], axis=1) for l in ls]).astype(f32)
    P["hy_bf"] = np.stack([np.stack([inp["hy_b1"][l], inp["hy_b2"][l], inp["hy_b3"][l], inp["hy_freq"][l]], axis=1) for l in ls]).astype(f32)
    P["hy_wo"] = np.asarray(inp["hy_w_out"][l0:l0 + L], f32)
    P["hy_dec"] = np.stack([np.asarray(inp["hy_decay"][l], f32).reshape(1, 1024) for l in ls])
    Lq = S
    t = np.linspace(0.0, 1.0, Lq, dtype=np.float32)
    w = (2.0 * np.float32(math.pi) * np.arange(Lq, dtype=np.float32) / np.float32(Lq)).astype(np.float32)
    fb = np.linspace(1e-4, 16 - 1, 16, dtype=np.float32)
    zz = (w[:, None] * fb[None, :]).astype(np.float32)
    feats = np.concatenate([t[:, None], np.cos(zz), -np.sin(zz)], axis=-1).astype(np.float32)
    P["c_featsT"] = np.ascontiguousarray(feats.T)
    P["c_tcol"] = np.ascontiguousarray(t.reshape(32, 128).T)
    m0 = np.ones((128, 1), f32)
    m0[0, 0] = 0.0
    P["c_m0"] = m0
    N = 2 * S
    a = np.arange(NF, dtype=np.int64)
    m = (a[:, None] * a[None, :]) % N
    ang = 2.0 * np.pi * np.arange(N, dtype=np.float64) / N
    ct, st_ = np.cos(ang), np.sin(ang)
    C = ct[m]
    Sn = st_[m]
    C[S + 1:, :] = 0.0
    C[:, S + 1:] = 0.0
    Sn[S + 1:, :] = 0.0
    Sn[:, S + 1:] = 0.0
    P["c_ctab"] = C.astype(ml_dtypes.bfloat16)
    P["c_stab"] = Sn.astype(ml_dtypes.bfloat16)
    wf = np.zeros(NF, f32)
    wf[0:S + 1] = 2.0 / N
    wf[0] = 1.0 / N
    wf[S] = 1.0 / N
    P["c_wf"] = np.ascontiguousarray(wf.reshape(33, 128).T)
    hp = np.full((64, 1), np.float32(math.pi / 2), f32)
    P["c_halfpi"] = hp


def _hy_setup(self, P):
    for k in ("hy_cw", "hy_cb", "hy_bias", "hy_w1", "hy_w23", "hy_bf", "hy_wo", "hy_dec", "c_featsT", "c_tcol", "c_m0",
              "c_ctab", "c_stab", "c_wf", "c_halfpi"):
        setattr(self, k, self.inp(k, P[k]))
    self.hyG = self.scratch("hyG", [2, 512, S], F32)


def _stage_hyena(self, l):
    nc, kb = self.nc, self.kb
    ps = self.ps
    TWO_PI = 2.0 * math.pi
    import os
    PH = os.environ.get('HY_PHASES', 'FCWI')
    with ExitStack() as es0:
        al = lambda n, sh, dt: es0.enter_context(self.sb(n, sh, dt))
        Aw = al("hy_A", [128, 33, 512], BF16)
        Bw = al("hy_B", [128, 33, 512], BF16)
        ctv = self.c_ctab.ap().rearrange("(tc p) f -> p tc f", p=128)
        stv = self.c_stab.ap().rearrange("(tc p) f -> p tc f", p=128)
        es_mid = ExitStack()
        alm = lambda n, sh, dt: es_mid.enter_context(self.sb(n, sh, dt))
        hstm = alm("hy_hstm", [128, 32, 512], BF16)
        hdtm = alm("hy_hdtm", [128, 32, 512], BF16)
        with ExitStack() as es1:
            al1 = lambda n, sh, dt: es1.enter_context(self.sb(n, sh, dt))
            feats = [al1("hy_feats%d" % i, [33, 512], F32) for i in range(2)]
            hA = al1("hy_hA", [64, 512], F32)
            hB = al1("hy_hB", [64, 512], F32)
            hC = al1("hy_hC", [64, 512], F32)
            h3b = al1("hy_h3b", [64, S], BF16)
            w1 = al1("hy_w1", [33, 64], F32)
            w23 = al1("hy_w23", [64, 2, 64], F32)
            bf = al1("hy_bf", [64, 4], F32)
            wo = al1("hy_wo", [64, 1024], F32)
            wob = al1("hy_wob", [64, 1024], BF16)
            dec = al1("hy_dec", [128, 1024], F32)
            tcol = al1("hy_tcol", [128, 32], F32)
            m0 = al1("hy_m0", [128, 1], F32)
            hpi = al1("hy_hpi", [64, 1], F32)
            Et = al1("hy_E", [128, 1024], F32)
            hf = al1("hy_hf", [128, 2, 512], F32)
            kb.dma("sp", w1[:, :], self.hy_w1.ap()[l])
            kb.dma("sp", w23[:, :, :], self.hy_w23.ap()[l])
            kb.dma("sp", bf[:, :], self.hy_bf.ap()[l])
            kb.dma("sp", wo[:, :], self.hy_wo.ap()[l])
            kb.copy("dve", wob[:, :], wo[:, :])
            kb.dma("sp", dec[:, :], row_bcast(self.hy_dec, l * 1024, 1024))
            kb.act(dec[:, :], dec[:, :], AF.Abs)
            kb.dma("sp", tcol[:, :], self.c_tcol.ap())
            kb.ts("dve", tcol[:, :], tcol[:, :], -1.0, op0=ALU.mult)
            kb.dma("sp", m0[:, :], self.c_m0.ap())
            kb.dma("sp", hpi[:, :], self.c_halfpi.ap())
            for tt in range(8 if 'F' in PH else 0):
                tsl = slice(tt * 512, (tt + 1) * 512)
                ft = feats[tt % 2]
                kb.dma("sp", ft[:, :], self.c_featsT.ap()[:, tsl])
                for li_ in range(3):
                    p_ = ps[(tt * 3 + li_) % 4]
                    if li_ == 0:
                        kb.mm(p_[0:64, :], w1[:, :], ft[:, :])
                    else:
                        kb.mm(p_[0:64, :], w23[:, li_ - 1, :], hA[:, :])
                    kb.ts("dve", hB[:, :], p_[0:64, :], bf[:, li_:li_ + 1], bf[:, 3:4], op0=ALU.add, op1=ALU.mult)
                    kb.ts("dve", hB[:, :], hB[:, :], -TWO_PI, TWO_PI, op0=ALU.max, op1=ALU.min)
                    kb.act(hC[:, :], hB[:, :], AF.Abs)
                    kb.act(hC[:, :], hC[:, :], AF.Sin, bias=hpi[:, 0:1], scale=-0.5)
                    kb.act(hB[:, :], hB[:, :], AF.Sin, scale=0.5)
                    if li_ < 2:
                        kb.stt("dve", hA[:, :], hB[:, :], 2.0, hC[:, :], ALU.mult, ALU.mult)
                    else:
                        kb.stt("dve", h3b[:, tsl], hB[:, :], 2.0, hC[:, :], ALU.mult, ALU.mult)
            for tc in range(32 if 'F' in PH else 0):
                csl = slice(tc * 128, (tc + 1) * 128)
                p0, p1 = ps[4 + 2 * (tc % 2)], ps[5 + 2 * (tc % 2)]
                kb.mm(p0[:, :], h3b[:, csl], wob[:, 0:512])
                kb.mm(p1[:, :], h3b[:, csl], wob[:, 512:1024])
                kb.act(Et[:, :], dec[:, :], AF.Exp, scale=tcol[:, tc:tc + 1])
                kb.tt("dve", hf[:, 0, :], p0[:, :], Et[:, 0:512], ALU.mult)
                kb.tt("dve", hf[:, 1, :], p1[:, :], Et[:, 512:1024], ALU.mult)
                if tc == 0:
                    kb.ts("dve", hf[:, 1, :], hf[:, 1, :], m0[:, 0:1], op0=ALU.mult)
                kb.tt("pool", (hstm[:, tc, :], tc), hf[:, 0, :], hf[:, 1, :], ALU.add)
                kb.tt("pool", (hdtm[:, tc, :], tc), hf[:, 0, :], hf[:, 1, :], ALU.subtract)
        kb.barrier()
        ztm = alm("hy_ztm", [128, 32, 512], BF16)
        with ExitStack() as es2:
            al2 = lambda n, sh, dt: es2.enter_context(self.sb(n, sh, dt))
            HS = S // 2
            U = al2("hy_U", [128, HS + 2], F32)
            a1 = al2("hy_a1", [128, HS], F32)
            a2 = al2("hy_a2", [128, HS], F32)
            a3 = al2("hy_a3", [128, HS], F32)
            cw = al2("hy_cw", [128, 12, 3], F32)
            cb = al2("hy_cb", [128, 12], F32)
            hb_ = al2("hy_bias", [128, 4], F32)
            kb.dma("sp", cw[:, :, :], self.hy_cw.ap()[l])
            kb.dma("sp", cb[:, :], self.hy_cb.ap()[l])
            kb.dma("sp", hb_[:, :], self.hy_bias.ap()[l])

            def conv(ch, dst, hf_):
                t0 = hf_ * HS
                rows = self.projT.ap()[ROW_BU + ch * 128:ROW_BU + (ch + 1) * 128, :]
                if hf_ == 0:
                    kb.memset("pool", U[:, 0:1], 0.0)
                    kb.dma("sp", U[:, 1:HS + 2], rows[:, 0:HS + 1])
                else:
                    kb.memset("pool", U[:, HS + 1:HS + 2], 0.0)
                    kb.dma("sp", U[:, 0:HS + 1], rows[:, t0 - 1:S])
                kb.ts("dve", dst[:, :], U[:, 0:HS], cw[:, ch, 0:1], cb[:, ch:ch + 1], op0=ALU.mult, op1=ALU.add)
                kb.stt("dve", dst[:, :], U[:, 1:HS + 1], cw[:, ch, 1:2], dst[:, :], ALU.mult, ALU.add)
                kb.stt("dve", dst[:, :], U[:, 2:HS + 2], cw[:, ch, 2:3], dst[:, :], ALU.mult, ALU.add)

            for cc in range(4 if 'C' in PH else 0):
                for hf_ in range(2):
                    t0 = hf_ * HS
                    conv(cc, a1, hf_)
                    conv(8 + cc, a2, hf_)
                    kb.tt("pool", a1[:, :], a1[:, :], a2[:, :], ALU.mult)
                    for g in range(4):
                        pt = ps[g % 4]
                        for i in range(4):
                            tc = g * 4 + i
                            kb.transpose(pt[:, i * 128:(i + 1) * 128], a1[:, tc * 128:(tc + 1) * 128], self.ident[:, :])
                        gg = hf_ * 4 + g
                        kb.copy("act", ztm[:, gg * 4:(gg + 1) * 4, cc * 128:(cc + 1) * 128], pt[:, :].rearrange("p (i c) -> p i c", c=128))
                    conv(4 + cc, a2, hf_)
                    kb.dma("sp", a3[:, :], self.projT.ap()[ROW_BZ + cc * 128:ROW_BZ + (cc + 1) * 128, t0:t0 + HS])
                    kb.tt("pool", a2[:, :], a2[:, :], a3[:, :], ALU.mult)
                    kb.stt("dve", a3[:, :], a1[:, :], hb_[:, cc:cc + 1], a2[:, :], ALU.mult, ALU.mult)
                    kb.dma("pool", self.hyG.ap()[0, cc * 128:(cc + 1) * 128, t0:t0 + HS], a2[:, :])
                    kb.dma("pool", self.hyG.ap()[1, cc * 128:(cc + 1) * 128, t0:t0 + HS], a3[:, :])
        kb.barrier()
        with ExitStack() as es3:
            al3 = lambda n, sh, dt: es3.enter_context(self.sb(n, sh, dt))
            slabs = [[al3("hy_ct%d" % i, [128, 16, 256], BF16), al3("hy_st%d" % i, [128, 16, 256], BF16)] for i in range(2)]
            wf = al3("hy_wf", [128, 33], F32)
            kc = al3("hy_kc", [128, 2, 512], F32)
            tq = al3("hy_tq", [128, 2, 512], F32)
            kb.dma("sp", wf[:, :], self.c_wf.ap())
            n_load = 0
            for fg in range(17 if 'W' in PH else 0):
                nfc = 2 if fg < 16 else 1
                f0 = fg * 256
                for half in range(2):
                    ct_, st2 = slabs[n_load % 2]
                    n_load += 1
                    for q4 in range(2):
                        kb.dma("sp", ct_[:, q4 * 8:(q4 + 1) * 8, 0:nfc * 128], ctv[:, half * 16 + q4 * 8:half * 16 + (q4 + 1) * 8, f0:f0 + nfc * 128])
                        kb.dma("sp", st2[:, q4 * 8:(q4 + 1) * 8, 0:nfc * 128], stv[:, half * 16 + q4 * 8:half * 16 + (q4 + 1) * 8, f0:f0 + nfc * 128])
                    for tc in range(16):
                        tcg = half * 16 + tc
                        first, last = (tcg == 0), (tcg == 31)
                        for i in range(nfc):
                            fsl = slice(i * 128, (i + 1) * 128)
                            kb.mm(ps[i * 4 + 0][:, :], ct_[:, tc, fsl], ztm[:, tcg, :], start=first, stop=last)
                            kb.mm(ps[i * 4 + 1][:, :], st2[:, tc, fsl], ztm[:, tcg, :], start=first, stop=last)
                            kb.mm(ps[i * 4 + 2][:, :], ct_[:, tc, fsl], (hstm[:, tcg, :], tcg), start=first, stop=last)
                            kb.mm(ps[i * 4 + 3][:, :], st2[:, tc, fsl], (hdtm[:, tcg, :], tcg), start=first, stop=last)
                for i in range(nfc):
                    fc = fg * 2 + i
                    zc, zs, kc_, ks_ = ps[i * 4 + 0], ps[i * 4 + 1], ps[i * 4 + 2], ps[i * 4 + 3]
                    kb.act(kc[:, 0, :], kc_[:, :], AF.Identity, scale=wf[:, fc:fc + 1])
                    kb.act(kc[:, 1, :], ks_[:, :], AF.Identity, scale=wf[:, fc:fc + 1])
                    kb.tt("dve", tq[:, 0, :], zc[:, :], kc[:, 0, :], ALU.mult)
                    kb.tt("dve", tq[:, 1, :], zs[:, :], kc[:, 1, :], ALU.mult)
                    kb.tt("pool", (Aw[:, fc, :], fc), tq[:, 0, :], tq[:, 1, :], ALU.subtract)
                    kb.tt("dve", tq[:, 0, :], zc[:, :], kc[:, 1, :], ALU.mult)
                    kb.tt("dve", tq[:, 1, :], zs[:, :], kc[:, 0, :], ALU.mult)
                    kb.tt("pool", (Bw[:, fc, :], fc), tq[:, 0, :], tq[:, 1, :], ALU.add)
        kb.barrier()
        es_mid.close()
        with ExitStack() as es4:
            al4 = lambda n, sh, dt: es4.enter_context(self.sb(n, sh, dt))
            slabs = [[al4("hy_cf%d" % i, [128, 33, 256], BF16), al4("hy_sf%d" % i, [128, 33, 256], BF16)] for i in range(2)]
            G = [al4("hy_G%d" % i, [128, 2, 4, 256], F32) for i in range(2)]
            yst = al4("hy_yst", [128, 4, S], BF16)
            tmp = al4("hy_tmp", [128, 2, 256], F32)
            gv = self.hyG.ap().rearrange("g (cc p) t -> p g cc t", p=128)
            for t16 in range(16 if 'I' in PH else 0):
                tsl = slice(t16 * 256, (t16 + 1) * 256)
                cf, sf = slabs[t16 % 2]
                for q3 in range(3):
                    kb.dma("sp", cf[:, q3 * 11:(q3 + 1) * 11, :], ctv[:, q3 * 11:(q3 + 1) * 11, tsl])
                    kb.dma("sp", sf[:, q3 * 11:(q3 + 1) * 11, :], stv[:, q3 * 11:(q3 + 1) * 11, tsl])
                g_ = G[t16 % 2]
                kb.dma("pool", g_[:, 0, :, :], gv[:, 0, :, tsl])
                kb.dma("pool", g_[:, 1, :, :], gv[:, 1, :, tsl])
                for cc in range(4):
                    p_ = ps[cc + 4 * (t16 % 2)]
                    csl = slice(cc * 128, (cc + 1) * 128)
                    for fc in range(33):
                        kb.mm(p_[:, 0:256], (Aw[:, fc, csl], fc), cf[:, fc, :], start=(fc == 0), stop=False)
                        kb.mm(p_[:, 0:256], (Bw[:, fc, csl], fc), sf[:, fc, :], start=False, stop=(fc == 32))
                    tm = (tmp[:, cc % 2, :], cc % 2)
                    kb.tt("dve", tm, p_[:, 0:256], g_[:, 0, cc, :], ALU.mult)
                    kb.tt("pool", yst[:, cc, tsl], tm, g_[:, 1, cc, :], ALU.add)
            kb.dma("pool", self.ybT.ap().rearrange("(cc p) t -> p cc t", p=128), yst[:, :, :])


Prog.hy_setup = _hy_setup
Prog.stage_hyena = _stage_hyena


RW_C = math.exp(-0.5)
BLK = 1024


def rwkv_host(inp, P, l0, L):
    f32 = np.float32
    ls = range(l0, l0 + L)
    P["rw_mu"] = np.stack([fm(inp["rw_mu"][l], 26) for l in ls])
    P["rw_w0"] = np.stack([np.asarray(inp["rw_w0"][l], f32).reshape(2, 8, 128).transpose(2, 0, 1) for l in ls])
    P["rw_a0"] = np.stack([np.asarray(inp["rw_a0"][l], f32).reshape(2, 8, 128).transpose(2, 0, 1) for l in ls])
    P["rw_w2"] = np.stack([np.asarray(inp["rw_w2"][l], f32).reshape(128, 1024) for l in ls])
    P["rw_a2"] = np.stack([np.asarray(inp["rw_a2"][l], f32).reshape(128, 1024) for l in ls])
    P["rw_vec"] = np.stack([np.stack([fm(inp["rw_kk"][l], 8), fm(inp["rw_ka"][l], 8), fm(np.asarray(inp["rw_rk"][l]).reshape(-1), 8),
                                      fm(inp["rw_lnx_g"][l], 8), fm(inp["rw_lnx_b"][l], 8)], axis=1) for l in ls])
    p_ = np.arange(128)
    P["c_rwmask"] = np.stack([(p_[None, :] < p_[:, None]), (p_[None, :] > p_[:, None]),
                              (p_[None, :] >= p_[:, None]), (p_[None, :] <= p_[:, None])], axis=1).astype(f32)
    lv = np.zeros((128, 14, 128), f32)
    for lev in range(7):
        m = 1 << lev
        mk = ((p_[:, None] // (2 * m) == p_[None, :] // (2 * m)) & (p_[:, None] % (2 * m) >= m) & (p_[None, :] % (2 * m) < m))
        lv[:, 2 * lev, :] = mk
        lv[:, 2 * lev + 1, :] = mk.T
    P["c_lvmask"] = lv.astype(ml_dtypes.bfloat16)
    P["c_identb"] = np.eye(128).astype(ml_dtypes.bfloat16)
    blk = np.zeros((128, 128), f32)
    blk[0:64, 0:64] = 1.0
    blk[64:128, 64:128] = 1.0
    P["c_blk"] = blk
    rst = np.ones((128, BLK), f32)
    rst[:, 0::128] = 0.0
    P["c_rst1k"] = rst


def _rw_setup(self, P):
    nc, kb = self.nc, self.kb
    for k in ("rw_mu", "rw_w0", "rw_a0", "rw_w2", "rw_a2", "rw_vec", "c_rwmask", "c_identb", "c_blk", "c_rst1k", "c_lvmask"):
        setattr(self, k, self.inp(k, P[k]))


def _stage_rwkv(self, l):
    nc, kb = self.nc, self.kb
    ps = self.ps
    import os
    NFC = int(os.environ.get("RW_NFC", "8"))
    with ExitStack() as es0:
        al = lambda n, sh, dt: es0.enter_context(self.sb(n, sh, dt))
        self.rwmask = al("rwmask", [128, 4, 128], F32)
        self.identb = al("identb", [128, 128], BF16)
        self.blkones = al("blkones", [128, 128], F32)
        self.rst1k = al("rst1k", [128, BLK], F32)
        self.lvmask = al("lvmask", [128, 14, 128], BF16)
        kb.dma("sp", self.lvmask[:, :, :], self.c_lvmask.ap())
        kb.dma("sp", self.rwmask[:, :, :], self.c_rwmask.ap())
        kb.dma("sp", self.identb[:, :], self.c_identb.ap())
        kb.dma("sp", self.blkones[:, :], self.c_blk.ap())
        kb.dma("sp", self.rst1k[:, :], self.c_rst1k.ap())
        mu = al("rw_mu", [128, 26], F32)
        mu1 = al("rw_mu1", [128, 26], F32)
        mu2 = al("rw_mu2", [128, 26], F32)
        w0 = al("rw_w0", [128, 2, 8], F32)
        a0 = al("rw_a0", [128, 2, 8], F32)
        vec = al("rw_vec", [128, 5, 8], F32)
        ka1 = al("rw_ka1", [128, 8], F32)
        wst = al("rw_wst", [128, 1024], F32)
        W2b = al("rw_W2b", [128, 1024], BF16)
        A2b = al("rw_A2b", [128, 1024], BF16)
        TW = al("rw_TW", [128, S], BF16)
        LA = al("rw_LA", [128, S], BF16)
        kb.dma("sp", mu[:, :], self.rw_mu.ap()[l])
        kb.ts("dve", mu1[:, :], mu[:, :], -1.0, 1.0, op0=ALU.mult, op1=ALU.add)
        kb.ts("dve", mu2[:, :], mu[:, :], 0.5, op0=ALU.mult)
        kb.dma("sp", w0[:, :, :], self.rw_w0.ap()[l])
        kb.dma("sp", a0[:, :, :], self.rw_a0.ap()[l])
        kb.dma("sp", vec[:, :, :], self.rw_vec.ap()[l])
        kb.ts("dve", ka1[:, :], vec[:, 1, :], -1.0, 1.0, op0=ALU.mult, op1=ALU.add)
        kb.dma("sp", wst[:, :], self.rw_w2.ap()[l])
        kb.copy("dve", W2b[:, :], wst[:, :])
        kb.dma("sp", wst[:, :], self.rw_a2.ap()[l])
        kb.copy("dve", A2b[:, :], wst[:, :])

        def shift(U, dst, ch, n):
            kb.tt("pool", dst, U[:, 0:n], U[:, 2:n + 2], ALU.add)
            kb.ts("dve", dst, dst, mu2[:, ch:ch + 1], op0=ALU.mult)
            kb.stt("dve", dst, U[:, 1:n + 1], mu1[:, ch:ch + 1], dst, ALU.mult, ALU.add)

        def load_halo(U, row0, t0, n):
            rows = self.projT.ap()[row0:row0 + 128, :]
            lo, hi = t0 - 1, t0 + n + 1
            if lo < 0:
                kb.memset("pool", U[:, 0:1], 0.0)
            if hi > S:
                kb.memset("pool", U[:, n + 1:n + 2], 0.0)
            a_, b_ = max(lo, 0), min(hi, S)
            kb.dma("sp", U[:, a_ - lo:b_ - lo], rows[:, a_:b_])

        with ExitStack() as es1:
            U = es1.enter_context(self.sb("rw_U0", [128, S + 2], F32))
            T_ = es1.enter_context(self.sb("rw_T0", [128, S], F32))
            load_halo(U, ROW_CU + 3072, 0, S)
            shift(U, T_[:, :], 24, S)
            kb.act(TW[:, :], T_[:, :], AF.Tanh)
            load_halo(U, ROW_CU + 3200, 0, S)
            shift(U, T_[:, :], 25, S)
            kb.copy("act", LA[:, :], T_[:, :])
        kb.barrier()
        with ExitStack() as es2:
            a2 = lambda n, sh, dt: es2.enter_context(self.sb(n, sh, dt))
            ysum = a2("rw_ysum", [128, 32, 128], F32)
            BON = a2("rw_bon", [128, S], F32)
            kd0 = a2("rw_kd0", [128, S], BF16)
            Ub = [a2("rw_U%d" % i, [128, BLK + 2], F32) for i in range(3)]
            rS, kS, vS = a2("rw_rS", [128, BLK], F32), a2("rw_kS", [128, BLK], F32), a2("rw_vS", [128, BLK], F32)
            SG, AL, LP = a2("rw_SG", [128, BLK], F32), a2("rw_AL", [128, BLK], F32), a2("rw_LP", [128, BLK], F32)
            Pt, iP, Pm = a2("rw_P", [128, BLK], F32), a2("rw_iP", [128, BLK], F32), a2("rw_Pm", [128, BLK], F32)
            kkn, tmpf = a2("rw_kkn", [128, BLK], F32), a2("rw_tmpf", [128, BLK], F32)
            RT, AT, BT, KT = (a2("rw_RT", [128, BLK], BF16), a2("rw_AT", [128, BLK], BF16),
                              a2("rw_BT", [128, BLK], BF16), a2("rw_KT", [128, BLK], BF16))
            vSb = a2("rw_vSb", [128, BLK], BF16)
            Btm, Ktm, Vtm = (a2("rw_Btm", [128, 8, 128], BF16), a2("rw_Ktm", [128, 8, 128], BF16), a2("rw_Vtm", [128, 8, 128], BF16))
            Gs = a2("rw_G", [128, 2, 8], F32)
            Xb = [a2("rw_X%d" % i, [128, 2, 128], BF16) for i in range(2)]
            XTb = [a2("rw_XT%d" % i, [128, 2, 128], BF16) for i in range(2)]
            TTb = [a2("rw_TT%d" % i, [128, 2, 128], BF16) for i in range(2)]
            Wa, Wb = a2("rw_Wa", [128, 2, 128], BF16), a2("rw_Wb", [128, 2, 128], BF16)
            MakT, NrbT, NrkT = (a2("rw_MakT", [128, 2, 128], BF16), a2("rw_NrbT", [128, 2, 128], BF16), a2("rw_NrkT", [128, 2, 128], BF16))
            RHS2, U2 = a2("rw_RHS2", [128, 128], BF16), a2("rw_U2", [128, 128], BF16)
            S2, S2b = a2("rw_S2", [128, 128], F32), a2("rw_S2b", [128, 128], BF16)
            stat = a2("rw_stat", [128, 2, 64], F32)
            yst = a2("rw_yst", [128, S], BF16)

            def mbc(mi):
                m_ = self.rwmask[:, mi, :]
                return bass.AP(m_.tensor, m_.offset, [list(m_.ap[0]), [0, 2], list(m_.ap[1])])

            idb = bass.AP(self.identb, 0, [list(self.identb[:, :].ap[0]), [0, 2], [1, 128]])
            v3 = lambda p_: p_[:, 0:256].rearrange("p (h s) -> p h s", s=128)

            for fcn in range(NFC):
                for d in range(2):
                    mX, mXT, mN = (0, 1, 2) if d == 0 else (1, 0, 3)
                    blocks = list(range(S // BLK)) if d == 0 else list(range(S // BLK - 1, -1, -1))
                    first = True
                    for bi in blocks:
                        t0 = bi * BLK
                        bsl = slice(t0, t0 + BLK)
                        for src_i, dst in enumerate((rS, kS, vS)):
                            load_halo(Ub[src_i], ROW_CU + src_i * 1024 + fcn * 128, t0, BLK)
                            shift(Ub[src_i], dst[:, :], src_i * 8 + fcn, BLK)
                        kb.copy("pool", vSb[:, :], vS[:, :])
                        for hf_ in range(2):
                            hsl = slice(hf_ * 512, (hf_ + 1) * 512)
                            gsl = slice(t0 + hf_ * 512, t0 + (hf_ + 1) * 512)
                            dsl = slice(d * 64, (d + 1) * 64)
                            kb.mm(ps[0][:, :], W2b[dsl, fcn * 128:(fcn + 1) * 128], TW[dsl, gsl])
                            kb.act(SG[:, hsl], ps[0][:, :], AF.Sigmoid, bias=w0[:, d, fcn:fcn + 1])
                            kb.mm(ps[1][:, :], A2b[dsl, fcn * 128:(fcn + 1) * 128], LA[dsl, gsl])
                            kb.act(AL[:, hsl], ps[1][:, :], AF.Sigmoid, bias=a0[:, d, fcn:fcn + 1])
                        kb.ts("dve", kkn[:, :], kS[:, :], vec[:, 0, fcn:fcn + 1], op0=ALU.mult)
                        kb.act(tmpf[:, :], kkn[:, :], AF.Square)
                        for hf_ in range(2):
                            hsl = slice(hf_ * 512, (hf_ + 1) * 512)
                            kb.mm(ps[2 + hf_][:, :], self.blkones[:, :], tmpf[:, hsl])
                            kb.rstd(tmpf[:, hsl], ps[2 + hf_][:, :], 1.0, 1e-24)
                        kb.tt("dve", kkn[:, :], kkn[:, :], tmpf[:, :], ALU.mult)
                        sg_, lp_, rs_ = SG[:, :], LP[:, :], self.rst1k[:, :]
                        kb.op("dve", lambda e, o=lp_, a=rs_, b=sg_: e.tensor_tensor_scan(out=o, data0=a, data1=b, initial=0.0, op0=ALU.mult, op1=ALU.add),
                              r=[SG[:, :], self.rst1k[:, :]], w=[LP[:, :]])
                        gv = bass.AP(LP, 127, [list(LP[:, :].ap[0]), [128, 8]])
                        kb.copy("dve", Gs[:, 0, :], gv)
                        kb.act(Gs[:, 1, :], Gs[:, 0, :], AF.Exp, scale=-RW_C)
                        if d == 1:
                            lp3 = LP[:, :].rearrange("p (j i) -> p j i", i=128)
                            kb.tt("dve", lp3, bc_last(Gs[:, 0, :], 128), lp3, ALU.subtract)
                            kb.tt("dve", LP[:, :], LP[:, :], SG[:, :], ALU.add)
                        kb.act(Pt[:, :], LP[:, :], AF.Exp, scale=-RW_C)
                        kb.act(iP[:, :], LP[:, :], AF.Exp, scale=RW_C)
                        kb.tt("pool", tmpf[:, :], LP[:, :], SG[:, :], ALU.subtract)
                        kb.act(Pm[:, :], tmpf[:, :], AF.Exp, scale=-RW_C)
                        kb.tt("dve", RT[:, :], rS[:, :], Pt[:, :], ALU.mult)
                        kb.stt("dve", AT[:, :], kkn[:, :], -1.0, Pm[:, :], ALU.mult, ALU.mult)
                        kb.tt("pool", tmpf[:, :], kkn[:, :], AL[:, :], ALU.mult)
                        kb.tt("pool", BT[:, :], tmpf[:, :], iP[:, :], ALU.mult)
                        kb.ts("dve", tmpf[:, :], AL[:, :], vec[:, 1, fcn:fcn + 1], ka1[:, fcn:fcn + 1], op0=ALU.mult, op1=ALU.add)
                        kb.tt("dve", tmpf[:, :], tmpf[:, :], kS[:, :], ALU.mult)
                        kb.tt("pool", KT[:, :], tmpf[:, :], iP[:, :], ALU.mult)
                        if d == 0:
                            kb.copy("act", kd0[:, bsl], tmpf[:, :])
                        else:
                            kb.tt("pool", tmpf[:, :], tmpf[:, :], kd0[:, bsl], ALU.add)
                            kb.stt("dve", tmpf[:, :], tmpf[:, :], vec[:, 2, fcn:fcn + 1], rS[:, :], ALU.mult, ALU.mult)
                            for hf_ in range(2):
                                hsl = slice(hf_ * 512, (hf_ + 1) * 512)
                                kb.mm(ps[2 + hf_][:, :], self.blkones[:, :], tmpf[:, hsl])
                                kb.stt("dve", BON[:, t0 + hf_ * 512:t0 + (hf_ + 1) * 512], ps[2 + hf_][:, :], 0.5, vS[:, hsl], ALU.mult, ALU.mult)
                        for src, dst in ((BT, Btm), (KT, Ktm), (vSb, Vtm)):
                            for g in range(2):
                                p_ = ps[4 + g]
                                for i in range(4):
                                    jl = g * 4 + i
                                    kb.mm(p_[:, i * 128:(i + 1) * 128], src[:, jl * 128:(jl + 1) * 128], self.identb[:, :])
                                kb.copy("act" if g == 0 else "dve", dst[:, g * 4:(g + 1) * 4, :], p_[:, :].rearrange("p (i c) -> p i c", c=128))
                        if self.debug and fcn == 0 and bi == blocks[0]:
                            for nm_, t_, dt_ in (("LP", LP, F32), ("kkn", kkn, F32), ("SG", SG, F32), ("AL", AL, F32), ("rS", rS, F32),
                                                 ("RT", RT, BF16), ("AT", AT, BF16), ("BT", BT, BF16), ("KT", KT, BF16)):
                                self.dump("rw_%s_d%d" % (nm_, d), t_[:, :], [128, BLK], dt_)
                        order = list(range(8)) if d == 0 else list(range(7, -1, -1))
                        RWS = os.environ.get("RW_STOP", "")
                        if "P" in RWS:
                            order = []
                        kb.pe_strict = True
                        for jl in order:
                            jg = bi * 8 + jl
                            csl = slice(jl * 128, (jl + 1) * 128)
                            pX, pXT, pT, pM, pN, pK, pR, pS_ = ps
                            for hh in range(2):
                                hs_ = slice(hh * 64, (hh + 1) * 64)
                                osl = slice(hh * 128, (hh + 1) * 128)
                                kb.mm(pX[:, osl], AT[hs_, csl], BT[hs_, csl])
                                kb.mm(pXT[:, osl], BT[hs_, csl], AT[hs_, csl])
                                kb.mm(pM[:, osl], KT[hs_, csl], AT[hs_, csl])
                                kb.mm(pN[:, osl], BT[hs_, csl], RT[hs_, csl])
                                kb.mm(pK[:, osl], KT[hs_, csl], RT[hs_, csl])
                            X, XT = Xb[0], XTb[0]
                            kb.tt("dve", X[:, :, :], v3(pX), mbc(mX), ALU.mult)
                            kb.tt("dve", XT[:, :, :], v3(pXT), mbc(mXT), ALU.mult)
                            kb.tt("dve", MakT[:, :, :], v3(pM), mbc(mXT), ALU.mult)
                            kb.tt("dve", NrbT[:, :, :], v3(pN), mbc(mN), ALU.mult)
                            kb.tt("dve", NrkT[:, :, :], v3(pK), mbc(mN), ALU.mult)

                            def lvm(i_):
                                m_ = self.lvmask[:, i_, :]
                                return bass.AP(m_.tensor, m_.offset, [list(m_.ap[0]), [0, 2], list(m_.ap[1])])

                            EA, EB = Xb[1], XTb[1]
                            Dm, DT = TTb[0], TTb[1]
                            ia, ib = (0, 1) if d == 0 else (1, 0)
                            kb.tt("pool", EA[:, :, :], X[:, :, :], lvm(ia), ALU.mult)
                            kb.tt("pool", EB[:, :, :], XT[:, :, :], lvm(ib), ALU.mult)
                            kb.tt("pool", Dm[:, :, :], EA[:, :, :], idb, ALU.add)
                            kb.tt("pool", DT[:, :, :], EB[:, :, :], idb, ALU.add)
                            NLEV = int(os.environ.get("RW_LEV", "7"))
                            for lev in range(1, NLEV):
                                kb.tt("pool", EA[:, :, :], X[:, :, :], lvm(2 * lev + ia), ALU.mult)
                                kb.tt("pool", EB[:, :, :], XT[:, :, :], lvm(2 * lev + ib), ALU.mult)
                                for hh in range(2):
                                    osl = slice(hh * 128, (hh + 1) * 128)
                                    kb.mm(pX[:, osl], EB[:, hh, :], Dm[:, hh, :])
                                    kb.mm(pXT[:, osl], EA[:, hh, :], DT[:, hh, :])
                                kb.copy("act", Wa[:, :, :], v3(pX))
                                kb.copy("act", Wb[:, :, :], v3(pXT))
                                for hh in range(2):
                                    osl = slice(hh * 128, (hh + 1) * 128)
                                    kb.mm(pT[:, osl], DT[:, hh, :], Wa[:, hh, :])
                                    kb.mm(pT[:, 256 + hh * 128:256 + (hh + 1) * 128], Dm[:, hh, :], Wb[:, hh, :])
                                kb.tt("dve", Dm[:, :, :], v3(pT), Dm[:, :, :], ALU.add)
                                kb.tt("dve", DT[:, :, :], pT[:, 256:512].rearrange("p (h s) -> p h s", s=128), DT[:, :, :], ALU.add)
                            TT = DT
                            if "R" in RWS:
                                continue
                            for hh in range(2):
                                hs_ = slice(hh * 64, (hh + 1) * 64)
                                if not first:
                                    kb.mm(pR[:, hs_], AT[hs_, csl], S2b[hs_, hs_], start=True, stop=False)
                                kb.mm(pR[:, hs_], MakT[:, hh, :], Vtm[:, jl, hs_], start=first, stop=True)
                            kb.copy("act", RHS2[:, :], pR[:, 0:128])
                            for hh in range(2):
                                hs_ = slice(hh * 64, (hh + 1) * 64)
                                kb.mm(pR[:, 128 + hh * 64:128 + (hh + 1) * 64], TT[:, hh, :], RHS2[:, hs_])
                            kb.copy("dve", U2[:, :], pR[:, 128:256])
                            for hh in range(2):
                                hs_ = slice(hh * 64, (hh + 1) * 64)
                                ysl = slice(256 + hh * 64, 256 + (hh + 1) * 64)
                                if not first:
                                    kb.mm(pR[:, ysl], RT[hs_, csl], S2b[hs_, hs_], start=True, stop=False)
                                kb.mm(pR[:, ysl], NrbT[:, hh, :], U2[:, hs_], start=first, stop=False)
                                kb.mm(pR[:, ysl], NrkT[:, hh, :], Vtm[:, jl, hs_], start=False, stop=True)
                            if d == 0:
                                kb.copy("act", (ysum[:, jg, :], jg), pR[:, 256:384])
                            else:
                                kb.tt("dve", (ysum[:, jg, :], jg), pR[:, 256:384], (ysum[:, jg, :], jg), ALU.add)
                            kb.mm(pS_[:, 0:128], Btm[:, jl, :], U2[:, :], start=True, stop=False)
                            kb.mm(pS_[:, 0:128], Ktm[:, jl, :], Vtm[:, jl, :], start=False, stop=True)
                            if first:
                                kb.copy("dve", S2[:, :], pS_[:, 0:128])
                            else:
                                kb.tt("dve", S2[:, :], pS_[:, 0:128], S2[:, :], ALU.add)
                            kb.ts("pool", S2[:, :], S2[:, :], Gs[:, 1, jl:jl + 1], op0=ALU.mult)
                            kb.copy("act", S2b[:, :], S2[:, :])
                            first = False
                if "F" in os.environ.get("RW_STOP", ""):
                    continue
                kb.pe_strict = False
                y4 = ysum[:, :, :].rearrange("p j (h i) -> p (j h) i", i=64)
                sq4 = SG[:, :].rearrange("p (a i) -> p a i", i=64)
                st0 = stat[:, 0, :]
                st1 = stat[:, 1, :]
                kb.op("dve", lambda e, o=st0, i_=y4: e.tensor_reduce(out=o, in_=i_, axis=AX.X, op=ALU.add), r=[ysum[:, :, :]], w=[stat[:, :, :]])
                kb.ts("dve", st0, st0, 1.0 / 64, op0=ALU.mult)
                kb.tt("dve", y4, y4, bc_last(st0, 64), ALU.subtract)
                for q in range(4):
                    yq = ysum[:, q * 8:(q + 1) * 8, :].rearrange("p j (h i) -> p (j h) i", i=64)
                    kb.act(sq4, yq, AF.Square)
                    s1q = stat[:, 1, q * 16:(q + 1) * 16]
                    kb.op("dve", lambda e, o=s1q, i_=sq4: e.tensor_reduce(out=o, in_=i_, axis=AX.X, op=ALU.add), r=[SG[:, :]], w=[stat[:, :, :]])
                kb.rstd(st1, st1, 1.0 / 64, 64e-5)
                kb.tt("pool", y4, y4, bc_last(st1, 64), ALU.mult)
                for g in range(8):
                    tsl = slice(g * 512, (g + 1) * 512)
                    pt = ps[g % 2]
                    for i in range(4):
                        kb.transpose(pt[:, i * 128:(i + 1) * 128], ysum[:, g * 4 + i, :], self.ident[:, :])
                    zt = Pt[:, 0:512]
                    kb.dma("sp", zt, self.projT.ap()[ROW_CZ + fcn * 128:ROW_CZ + (fcn + 1) * 128, tsl])
                    kb.ts("dve", tmpf[:, 0:512], pt[:, :], vec[:, 3, fcn:fcn + 1], vec[:, 4, fcn:fcn + 1], op0=ALU.mult, op1=ALU.add)
                    kb.tt("pool", tmpf[:, 0:512], tmpf[:, 0:512], BON[:, tsl], ALU.add)
                    kb.tt("pool", yst[:, tsl], tmpf[:, 0:512], zt, ALU.mult)
                kb.dma("pool", self.ycT.ap()[fcn * 128:(fcn + 1) * 128, :], yst[:, :])


Prog.rw_setup = _rw_setup
Prog.stage_rwkv = _stage_rwkv


def build_program(P, n_layers=DEPTH, debug=False):
    pr = Prog(n_layers=n_layers, debug=debug)
    pr.setup_common(P)
    pr.ml_setup(P)
    pr.hy_setup(P)
    pr.rw_setup(P)
    kb = pr.kb
    for l in range(n_layers):
        for st in (pr.stage_mod, pr.stage_inproj, pr.stage_mlstm, pr.stage_hyena, pr.stage_rwkv, pr.stage_final):
            st(l)
            kb.barrier()
    pr.finish()
    return pr


def kernel(**inputs):
    inputs = {k: np.asarray(v) for k, v in inputs.items()}
    P0 = host_inputs(inputs, 0, n_layers=DEPTH)
    pr = build_program(P0, n_layers=DEPTH, debug=False)
    in_maps = []
    for b in range(NCORES):
        m = dict(pr.inputs)
        m["xT"] = np.ascontiguousarray(np.asarray(inputs["x"][b], np.float32).T)
        m["c_fm"] = fm(inputs["c"][b], 8)
        in_maps.append(m)
    res = run_bass_kernel_spmd(pr.nc, in_maps, core_ids=list(range(NCORES)))
    out = np.stack([np.ascontiguousarray(np.asarray(res.results[b]["outT"], np.float32).T) for b in range(NCORES)])
    return out.astype(np.float32)
```
